# Optimizing a Trainium2 kernel written in Bass

```python
import math
import jax
import jax.numpy as jnp
from jax import lax
import numpy as np

D_MODEL = 1024
BATCH = 8
SEQ = 2048
DEPTH = 1
DEC_BATCH = 128
DEC_SEQ = 4
PAST_LEN = 16384
PAGE_SIZE = 128

A_HEADS = 8
A_DK = 128
A_DV = D_MODEL // A_HEADS
A_WIDTH_K = A_HEADS * A_DK
A_WIDTH = A_HEADS * A_DV
A_CHUNK = 64
B_HEADS = 16
B_KV_HEADS = 2
B_GROUP = B_HEADS // B_KV_HEADS
B_HD = 64
B_WIDTH = B_HEADS * B_HD
B_KV_WIDTH = B_KV_HEADS * B_HD
WINDOW = 128
REL_BUCKETS = 32
REL_MAX_DIST = 128
PLE_DIM = 256
EPS = 1e-6

IN_SPLITS = (A_WIDTH_K, A_WIDTH_K, A_WIDTH, A_WIDTH, A_WIDTH,
             B_WIDTH, B_KV_WIDTH, B_KV_WIDTH, B_WIDTH, D_MODEL, D_MODEL)
IN_TOTAL = 2 * A_WIDTH_K + 3 * A_WIDTH + 2 * B_WIDTH + 2 * B_KV_WIDTH + 2 * D_MODEL

kernel_name = 'hgrn2_swa_sink_gated_parallel_decoder_step'


def rmsnorm(x, g):
    xf = x.astype(jnp.float32)
    y = xf * lax.rsqrt(jnp.mean(xf * xf, axis=-1, keepdims=True) + EPS)
    return (y * g.astype(jnp.float32)).astype(x.dtype)


def rel_bucket(rel):
    n = jnp.maximum(rel, 0)
    max_exact = REL_BUCKETS // 2
    nf = jnp.maximum(n, 1).astype(jnp.float32)
    large = max_exact + (jnp.log(nf / max_exact) / math.log(REL_MAX_DIST / max_exact)
                         * (REL_BUCKETS - max_exact)).astype(jnp.int32)
    large = jnp.minimum(large, REL_BUCKETS - 1)
    return jnp.where(n < max_exact, n, large)


def rel_bias_heads(rel, rel_bias):
    b = rel_bias.astype(jnp.float32)[rel_bucket(rel)]
    q_len, k_len = rel.shape
    return jnp.transpose(b, (2, 0, 1)).reshape(B_KV_HEADS, B_GROUP, q_len, k_len)


def sink_softmax(logits, mask, sink):
    logits = jnp.where(mask, logits, -jnp.inf)
    m = jnp.maximum(jnp.max(logits, axis=-1, keepdims=True), sink)
    p = jnp.exp(logits - m)
    return p / (jnp.sum(p, axis=-1, keepdims=True) + jnp.exp(sink - m))


def hgrn2_chunked(q, k, v, logf, s0, chunk):
    bn, L, h, _ = q.shape
    nc = L // chunk

    def to_chunks(a):
        return jnp.transpose(a.reshape(bn, nc, chunk, h, a.shape[-1]), (1, 0, 3, 2, 4))

    causal = jnp.tril(jnp.ones((chunk, chunk), dtype=bool))

    def step(S, inp):
        qc, kc, vc, gc = inp
        b = jnp.cumsum(gc, axis=2)
        o_inter = jnp.einsum('bhtd,bhdv->bhtv', qc * jnp.exp(b), S)
        diff = b[:, :, :, None, :] - b[:, :, None, :, :]
        decay = jnp.exp(jnp.where(causal[:, :, None], diff, -jnp.inf))
        att = jnp.einsum('bhtd,bhsd,bhtsd->bhts', qc, kc, decay)
        o_intra = jnp.einsum('bhts,bhsv->bhtv', att, vc)
        b_last = b[:, :, -1]
        k_state = kc * jnp.exp(b_last[:, :, None] - b)
        S_new = jnp.exp(b_last)[..., None] * S + jnp.einsum('bhsd,bhsv->bhdv', k_state, vc)
        return S_new, o_inter + o_intra

    s_fin, o = lax.scan(step, s0, (to_chunks(q), to_chunks(k), to_chunks(v), to_chunks(logf)))
    o = jnp.transpose(o, (1, 0, 3, 2, 4)).reshape(bn, L, h, v.shape[-1])
    return o, s_fin


def hgrn2_branch(q_pre, f_pre, i_pre, og_pre, s0, lb, norm_g):
    bn, L, _ = q_pre.shape
    f32 = jnp.float32
    q = jax.nn.silu(q_pre.astype(f32)).reshape(bn, L, A_HEADS, A_DK)
    fg = lb + (1.0 - lb) * jax.nn.sigmoid(f_pre.astype(f32))
    logf = jnp.log(fg).reshape(bn, L, A_HEADS, A_DK)
    k = (1.0 - fg).reshape(bn, L, A_HEADS, A_DK)
    v = i_pre.astype(f32).reshape(bn, L, A_HEADS, A_DV)
    chunk = A_CHUNK if L % A_CHUNK == 0 else L
    o, s_new = hgrn2_chunked(q, k, v, logf, s0.astype(f32), chunk)
    o = o * lax.rsqrt(jnp.mean(o * o, axis=-1, keepdims=True) + EPS)
    o = o * norm_g.astype(f32).reshape(A_HEADS, A_DV)
    o = o.reshape(bn, L, A_WIDTH) * jax.nn.sigmoid(og_pre.astype(f32))
    return o, s_new


def swa_prompt(q, k, v, sink, rel_bias):
    f32 = jnp.float32
    bn, L = q.shape[0], q.shape[1]
    blk = WINDOW
    nb = L // blk
    qb = q.astype(f32).reshape(bn, nb, blk, B_KV_HEADS, B_GROUP, B_HD)
    kb = k.astype(f32).reshape(bn, nb, blk, B_KV_HEADS, B_HD)
    vb = v.astype(f32).reshape(bn, nb, blk, B_KV_HEADS, B_HD)
    zero = jnp.zeros_like(kb[:, :1])
    kk = jnp.concatenate([jnp.concatenate([zero, kb[:, :-1]], axis=1), kb], axis=2)
    vv = jnp.concatenate([jnp.concatenate([zero, vb[:, :-1]], axis=1), vb], axis=2)
    rel = jnp.arange(blk)[:, None] + blk - jnp.arange(2 * blk)[None, :]
    band = (rel >= 0) & (rel < WINDOW)
    kvalid = (jnp.arange(nb)[:, None] * blk - blk + jnp.arange(2 * blk)[None, :]) >= 0
    mask = (band[None] & kvalid[:, None, :])[None, :, None, None]
    logits = jnp.einsum('bnqkgd,bnskd->bnkgqs', qb, kk) * (B_HD ** -0.5)
    logits = logits + rel_bias_heads(rel, rel_bias)
    probs = sink_softmax(logits, mask, sink.astype(f32).reshape(B_KV_HEADS, B_GROUP, 1, 1))
    out = jnp.einsum('bnkgqs,bnskd->bnqkgd', probs, vv)
    return out.reshape(bn, L, B_WIDTH), k[:, -WINDOW:], v[:, -WINDOW:]


def swa_sample(q, k, v, win_k, win_v, sink, rel_bias):
    f32 = jnp.float32
    bn, T = q.shape[0], q.shape[1]
    W = win_k.shape[1]
    kk = jnp.concatenate([win_k.astype(k.dtype), k], axis=1)
    vv = jnp.concatenate([win_v.astype(v.dtype), v], axis=1)
    rel = (W + jnp.arange(T))[:, None] - jnp.arange(W + T)[None, :]
    mask = (rel >= 0) & (rel < WINDOW)
    qg = q.astype(f32).reshape(bn, T, B_KV_HEADS, B_GROUP, B_HD)
    logits = jnp.einsum('bqkgd,bskd->bkgqs', qg, kk.astype(f32)) * (B_HD ** -0.5)
    logits = logits + rel_bias_heads(rel, rel_bias)
    probs = sink_softmax(logits, mask, sink.astype(f32).reshape(B_KV_HEADS, B_GROUP, 1, 1))
    out = jnp.einsum('bkgqs,bskd->bqkgd', probs, vv.astype(f32))
    return out.reshape(bn, T, B_WIDTH), kk[:, T:], vv[:, T:]


def trunk_layer(x, pl, s_hgrn, win_k, win_v, layer, norm_pre, w_in, hgrn_lb, hgrn_norm,
                attn_sink, rel_bias, w_pa, w_pb, w_o, norm_post, w_ple, w_ple_gate):
    bn, L, _ = x.shape
    u = rmsnorm(x, norm_pre)
    proj = jnp.einsum('bld,de->ble', u, w_in)
    offs = np.cumsum(IN_SPLITS)[:-1].tolist()
    aq, af, ai, aog, az, bq, bk, bv, bz, ga, gb = jnp.split(proj, offs, axis=-1)
    lb = jnp.cumsum(jax.nn.softmax(hgrn_lb.astype(jnp.float32), axis=0), axis=0)[layer]
    if s_hgrn is None:
        s_hgrn = jnp.zeros((bn, A_HEADS, A_DK, A_DV), jnp.float32)
    oa, s_new = hgrn2_branch(aq, af, ai, aog, s_hgrn, lb, hgrn_norm)
    oa = (oa * jax.nn.silu(az.astype(jnp.float32))).astype(x.dtype)
    a = jnp.einsum('ble,ed->bld', oa, w_pa)
    qb = bq.reshape(bn, L, B_HEADS, B_HD)
    kb = bk.reshape(bn, L, B_KV_HEADS, B_HD)
    vb = bv.reshape(bn, L, B_KV_HEADS, B_HD)
    if win_k is None:
        ob, k_win, v_win = swa_prompt(qb, kb, vb, attn_sink, rel_bias)
    else:
        ob, k_win, v_win = swa_sample(qb, kb, vb, win_k, win_v, attn_sink, rel_bias)
    ob = (ob * jax.nn.silu(bz.astype(jnp.float32))).astype(x.dtype)
    b = jnp.einsum('ble,ed->bld', ob, w_pb)
    m = jax.nn.sigmoid(ga) * a + jax.nn.sigmoid(gb) * b
    y = jnp.einsum('bld,de->ble', m, w_o)
    x = x + rmsnorm(y, norm_post)
    e = jnp.einsum('blp,pd->bld', pl, w_ple) * jax.nn.sigmoid(jnp.einsum('bld,de->ble', x, w_ple_gate))
    x = x + e
    return x, s_new, k_win, v_win


def setup_inputs(seed: int = 0) -> dict:
    key = jax.random.key(seed)
    ks = jax.random.split(key, 20)
    f32 = jnp.float32

    def nrm(k, shape, s):
        return jax.random.normal(k, shape, f32) * s

    return {
        'x_prompt': nrm(ks[0], (BATCH, SEQ, D_MODEL), 1.0),
        'x_sample': nrm(ks[1], (DEC_BATCH, DEC_SEQ, D_MODEL), 1.0),
        'state_hgrn': nrm(ks[2], (DEPTH, DEC_BATCH, A_HEADS, A_DK, A_DV), 0.5),
        'cache_swa_k': nrm(ks[3], (DEPTH, DEC_BATCH, WINDOW, B_KV_HEADS, B_HD), 1.0),
        'cache_swa_v': nrm(ks[4], (DEPTH, DEC_BATCH, WINDOW, B_KV_HEADS, B_HD), 1.0),
        'p_prompt': nrm(ks[5], (DEPTH, BATCH, SEQ, PLE_DIM), 1.0),
        'p_sample': nrm(ks[6], (DEPTH, DEC_BATCH, DEC_SEQ, PLE_DIM), 1.0),
        'norm_pre': 1.0 + nrm(ks[7], (DEPTH, D_MODEL), 0.05),
        'w_in': nrm(ks[8], (DEPTH, D_MODEL, IN_TOTAL), D_MODEL ** -0.5),
        'hgrn_lb': nrm(ks[9], (DEPTH + 1, A_WIDTH_K), 0.5),
        'hgrn_norm': 1.0 + nrm(ks[10], (DEPTH, A_WIDTH), 0.05),
        'attn_sink': nrm(ks[11], (DEPTH, B_HEADS), 0.5),
        'rel_bias': nrm(ks[12], (REL_BUCKETS, B_HEADS), 0.5),
        'w_pa': nrm(ks[13], (DEPTH, A_WIDTH, D_MODEL), A_WIDTH ** -0.5),
        'w_pb': nrm(ks[14], (DEPTH, B_WIDTH, D_MODEL), B_WIDTH ** -0.5),
        'w_o': nrm(ks[15], (DEPTH, D_MODEL, D_MODEL), D_MODEL ** -0.5),
        'norm_post': 1.0 + nrm(ks[16], (DEPTH, D_MODEL), 0.05),
        'w_ple': nrm(ks[17], (DEPTH, PLE_DIM, D_MODEL), PLE_DIM ** -0.5),
        'w_ple_gate': nrm(ks[18], (DEPTH, D_MODEL, D_MODEL), D_MODEL ** -0.5),
    }


def reference(x_prompt, x_sample, state_hgrn, cache_swa_k, cache_swa_v, p_prompt, p_sample,
              norm_pre, w_in, hgrn_lb, hgrn_norm, attn_sink, rel_bias, w_pa, w_pb, w_o,
              norm_post, w_ple, w_ple_gate):
    xp, xs = x_prompt, x_sample
    sp_list, ss_list, kp_list, vp_list, ks_list, vs_list = [], [], [], [], [], []
    for l in range(DEPTH):
        w = (norm_pre[l], w_in[l], hgrn_lb, hgrn_norm[l], attn_sink[l], rel_bias,
             w_pa[l], w_pb[l], w_o[l], norm_post[l], w_ple[l], w_ple_gate[l])
        xp, sp, kp, vp = trunk_layer(xp, p_prompt[l], None, None, None, l, *w)
        xs, ss, ksm, vsm = trunk_layer(xs, p_sample[l], state_hgrn[l], cache_swa_k[l],
                                       cache_swa_v[l], l, *w)
        sp_list.append(sp.astype(x_prompt.dtype))
        ss_list.append(ss.astype(state_hgrn.dtype))
        kp_list.append(kp)
        vp_list.append(vp)
        ks_list.append(ksm.astype(cache_swa_k.dtype))
        vs_list.append(vsm.astype(cache_swa_v.dtype))
    state_hgrn_prompt = jnp.stack(sp_list)
    state_hgrn_sample = jnp.stack(ss_list)
    swa_k_prompt = jnp.stack(kp_list)
    swa_v_prompt = jnp.stack(vp_list)
    swa_k_sample = jnp.stack(ks_list)
    swa_v_sample = jnp.stack(vs_list)
    return (xp, xs, state_hgrn_prompt, state_hgrn_sample, swa_k_prompt, swa_v_prompt,
            swa_k_sample, swa_v_sample)
```

```python
import math
import numpy as np
import concourse.bass as bass
import concourse.mybir as mybir
from concourse.bass_utils import run_bass_kernel_spmd

F32 = mybir.dt.float32
BF16 = mybir.dt.bfloat16
I32 = mybir.dt.int32
AF = mybir.ActivationFunctionType
ALU = mybir.AluOpType
AX = mybir.AxisListType
EPS = 1e-6
NT = 512
NTILES = 4
USE_POOL_DIV = False
FOLD_WAIT = True
STRICT_SAME_ENGINE = True
MARKS = []
IN_TOTAL = 9472
C_AQ, C_AF, C_AI, C_AOG, C_AZ, C_BQ, C_BK, C_BV, C_BZ, C_GA, C_GB = (
    0, 1024, 2048, 3072, 4096, 5120, 6144, 6272, 6400, 7424, 8448)


class DSem:
    def __init__(self, h):
        self.h = h
        self.total = 0


class T:
    def __init__(self, t, name=""):
        self.t = t
        self.name = name
        self.w = None
        self.r = {}


class Ctx:
    def __init__(self, nc):
        self.nc = nc
        self.eng = {"pe": nc.tensor, "act": nc.scalar, "dve": nc.vector,
                    "pool": nc.gpsimd, "sp": nc.sync}
        self.sem = {}
        self.cnt = {}
        for e in ("pe", "act", "dve", "pool"):
            self.sem[e] = nc.alloc_semaphore("c_" + e)
            self.cnt[e] = 0
        self.seen = {e: {} for e in self.eng}
        self.dsems = []
        self.nwait = 0
        self.nins = {e: 0 for e in self.eng}
        self.redirect = False

    def sb(self, name, shape, dtype):
        return T(self.nc.alloc_sbuf_tensor(name, list(shape), dtype), name)

    def dsem(self, name):
        d = DSem(self.nc.alloc_semaphore("d_" + name))
        self.dsems.append(d)
        return d

    def _need(self, needs, key, val):
        if isinstance(key, DSem):
            needs[key] = key.total
        elif needs.get(key, 0) < val:
            needs[key] = val

    def _sync(self, e, reads, writes, fold_ok=False):
        needs = {}
        for t in reads:
            if t.w is not None:
                self._need(needs, t.w[0], t.w[1])
        same_ok = (e == "pe") or not STRICT_SAME_ENGINE
        for t in writes:
            if t.w is not None and (t.w[0] != e or not same_ok):
                self._need(needs, t.w[0], t.w[1])
            for k, v in t.r.items():
                if k != e or not same_ok:
                    self._need(needs, k, v)
        eng = self.eng[e]
        pend = [(k, v) for k, v in needs.items() if self.seen[e].get(k, 0) < v]
        fold = None
        if fold_ok and FOLD_WAIT and pend:
            fold = pend.pop()
        for k, v in pend:
            eng.wait_ge(k.h if isinstance(k, DSem) else self.sem[k], v)
            self.seen[e][k] = v
            self.nwait += 1
        if fold is not None:
            self.seen[e][fold[0]] = fold[1]
            return (fold[0].h if isinstance(fold[0], DSem) else self.sem[fold[0]], fold[1])
        return None

    def op(self, e, fn, reads=(), writes=(), signal=True, nofold=False):
        if e == "pool" and self.redirect:
            e = "dve"
        fw = self._sync(e, reads, writes, fold_ok=(e in ("act", "dve", "pool") and not nofold))
        ins = fn(self.eng[e])
        if fw is not None:
            ins._wait_ge(fw[0], fw[1])
        self.nins[e] += 1
        if signal:
            self.cnt[e] += 1
            ins.then_inc(self.sem[e], 1)
            val = self.cnt[e]
        else:
            val = self.cnt[e] + 1
        for t in reads:
            if t.r.get(e, 0) < val:
                t.r[e] = val
        for t in writes:
            t.w = (e, val)
            t.r = {}
        return ins

    def dma(self, q, out_ap, in_ap, ds, reads=(), writes=(), **kw):
        self._sync(q, reads, writes)
        ins = self.eng[q].dma_start(out=out_ap, in_=in_ap, **kw)
        self.nins[q] += 1
        ins.then_inc(ds.h, 16)
        ds.total += 16
        for t in reads:
            t.r[ds] = None
        for t in writes:
            t.w = (ds, None)
            t.r = {}
        return ins

    def finish(self):
        sp = self.eng["sp"]
        for d in self.dsems:
            if d.total > 0:
                sp.wait_ge(d.h, d.total)
        for e in ("pe", "act", "dve", "pool"):
            if self.cnt[e] > 0:
                sp.wait_ge(self.sem[e], self.cnt[e])


def bc(a, n):
    return bass.AP(a.tensor, a.offset, [list(d) for d in a.ap] + [[0, n]])


def bcmid(a, n):
    d = [list(x) for x in a.ap]
    return bass.AP(a.tensor, a.offset, d[:-1] + [[0, n]] + d[-1:])


def rawap(t, off, dims):
    return bass.AP(t, off, [list(d) for d in dims])


def build_program():
    nc = bass.Bass("TRN2", target_bir_lowering=False)
    c = Ctx(nc)
    STOP = 99
    NTL = NTILES

    def DI(name, shape, dt=F32):
        return nc.dram_tensor(name, list(shape), dt, kind="ExternalInput")

    def DO(name, shape, dt=F32):
        return nc.dram_tensor(name, list(shape), dt, kind="ExternalOutput")

    def DS(name, shape, dt):
        return T(nc.dram_tensor(name, list(shape), dt, kind="Internal"), name)

    xp = DI("xp", [2048, 1024]); pp = DI("pp", [2048, 256])
    xsm = DI("xsm", [64, 1024]); psm = DI("psm", [64, 256])
    st0 = DI("st0", [16, 8, 128, 128]); ck = DI("ck", [16, 128, 128]); cv = DI("cv", [16, 128, 128])
    w_in = DI("w_in", [1024, IN_TOTAL]); w_ab = DI("w_ab", [1024, 2048])
    w_og = DI("w_og", [1024, 2048])
    w_ple = DI("w_ple", [256, 1024])
    norm_pre = DI("norm_pre", [1, 1024]); hgrn_lb = DI("hgrn_lb", [2, 1024]); hgrn_norm = DI("hgrn_norm", [1, 1024])
    attn_sink = DI("attn_sink", [1, 16]); rel_bias = DI("rel_bias", [32, 16]); norm_post = DI("norm_post", [1, 1024])
    k_ident = DI("k_ident", [128, 128]); k_mask64 = DI("k_mask64", [128, 128]); k_scanm = DI("k_scanm", [128, 512])
    k_grev = DI("k_grev", [32, 384]); k_valid = DI("k_valid", [16, 384])
    k_bmask = DI("k_bmask", [64, 16]); k_mask4 = DI("k_mask4", [64, 64])
    yp = DO("yp", [2048, 1024]); ysm = DO("ysm", [64, 1024])
    sp_o = DO("sp_o", [8, 128, 128]); ss_o = DO("ss_o", [16, 8, 128, 128])
    kp_o = DO("kp_o", [128, 128]); vp_o = DO("vp_o", [128, 128])
    ks_o = DO("ks_o", [16, 128, 128]); vs_o = DO("vs_o", [16, 128, 128])
    wb_all = nc.dram_tensor("wb_all", [1024, IN_TOTAL], BF16, kind="Internal")
    wb_in = [T(wb_all, "wb_in%d" % i) for i in range(10)]
    wb_ab = DS("wb_ab", [1024, 2048], BF16); wb_og = DS("wb_og", [1024, 2048], BF16)
    wb_ple = DS("wb_ple", [256, 1024], BF16)
    escr = DS("escr", [16, 384], F32)

    GB = [0, C_AI, C_AOG, C_BQ, C_GA, IN_TOTAL]
    NG = len(GB) - 1

    def grp_of(col):
        for q_ in range(NG):
            if GB[q_] <= col < GB[q_ + 1]:
                return q_
        raise ValueError(col)

    wb_grp = [T(wb_all, "wb_grp%d" % q) for q in range(NG)]
    cw_ds = [c.dsem("cw%d" % q) for q in range(NG)]
    cw_other = {d.name: c.dsem("cw_" + d.name) for d in (wb_ab, wb_og, wb_ple)}
    cast_done = set()

    def cast_grp(q):
        if ("g", q) in cast_done:
            return
        cast_done.add(("g", q))
        for r in range(8):
            c.dma("pool", wb_all[r * 128:(r + 1) * 128, GB[q]:GB[q + 1]], w_in[r * 128:(r + 1) * 128, GB[q]:GB[q + 1]], cw_ds[q], writes=[wb_grp[q]])

    def cast_w(src, dst):
        if dst.name in cast_done:
            return
        cast_done.add(dst.name)
        for r in range(src.shape[0] // 128):
            c.dma("pool", dst.t[r * 128:(r + 1) * 128, :], src[r * 128:(r + 1) * 128, :], cw_other[dst.name], writes=[dst])

    def cast_stage(k):
        if k == 0:
            for q in range(NG):
                cast_grp(q)
            cast_w(w_ab, wb_ab); cast_w(w_og, wb_og); cast_w(w_ple, wb_ple)

    banks = [T(nc.alloc_psum_tensor("bank%d" % i, [128, 512], F32), "bank%d" % i) for i in range(8)]
    bstate = {"i": 0, "n": 5}

    def P():
        lst = bstate.get("list")
        if lst is not None:
            b = banks[lst[bstate["i"] % len(lst)]]
        else:
            b = banks[bstate["i"] % bstate["n"]]
        bstate["i"] += 1
        return b

    def bfv(b):
        return b.t[:, :].bitcast(BF16)

    d_const = c.dsem("const")
    ident = c.sb("ident", [128, 128], BF16)
    mask64 = c.sb("mask64", [128, 128], BF16)
    scanm = c.sb("scanm", [128, 512], F32)
    gpre = c.sb("gpre", [128, 1024], F32); gpost = c.sb("gpost", [128, 1024], F32); gn125 = c.sb("gn125", [128, 1024], F32)
    esink = c.sb("esink", [128, 16], F32)
    neghalf = c.sb("neghalf", [128, 16], F32)
    lbt = c.sb("lbt", [128, 2, 8], F32); th = c.sb("th", [128, 8], F32)
    c0 = c.sb("c0", [128, 8], F32); c1 = c.sb("c1", [128, 8], F32); nc1 = c.sb("nc1", [128, 8], F32)
    relb = c.sb("relb", [32, 16], F32); grev = c.sb("grev", [32, 384], F32); valid = c.sb("valid", [16, 384], F32)
    ev = c.sb("ev", [16, 384], F32)
    Et = c.sb("Et", [128, 16, 256], BF16)
    tmpf = [c.sb("tmpf%d" % i, [128, 512], F32) for i in range(8)]
    tstate = {"i": 0}

    def TF():
        t = tmpf[tstate["i"] % len(tmpf)]
        tstate["i"] += 1
        return t

    f1k = [c.sb("f1k%d" % i, [128, 1024], F32) for i in range(2)]
    b1k = [c.sb("b1k%d" % i, [128, 1024], BF16) for i in range(3)]
    fstate = {"f": 0, "b": 0}

    def F1K():
        t = f1k[fstate["f"] % 2]; fstate["f"] += 1; return t

    def B1K():
        t = b1k[fstate["b"] % 3]; fstate["b"] += 1; return t

    yo = f1k
    dyo = [c.dsem("yo%d" % i) for i in range(2)]
    ones128 = c.sb("ones128", [128, 128], BF16)
    bmask = c.sb("bmask", [64, 16], F32)
    mask4 = c.sb("mask4", [64, 64], BF16)
    gnT = c.sb("gnT", [128, 8], F32)
    esinkP2 = c.sb("esinkP2", [128, 8], F32)
    Rrep = c.sb("Rrep", [64, 16, 4], F32)

    c.dma("sp", scanm.t[:, :], k_scanm[:, :], d_const, writes=[scanm])
    c.dma("sp", gpre.t[:, :], rawap(norm_pre, 0, [[0, 128], [1, 1024]]), d_const, writes=[gpre])
    c.dma("sp", gpost.t[:, :], rawap(norm_post, 0, [[0, 128], [1, 1024]]), d_const, writes=[gpost])
    c.dma("sp", gn125.t[:, :], rawap(hgrn_norm, 0, [[0, 128], [1, 1024]]), d_const, writes=[gn125])
    c.dma("sp", esink.t[:, :], rawap(attn_sink, 0, [[0, 128], [1, 16]]), d_const, writes=[esink])
    c.dma("sp", lbt.t[:, :, :], rawap(hgrn_lb, 0, [[1, 128], [1024, 2], [128, 8]]), d_const, writes=[lbt],
          allow_slow_non_contiguous=True)
    c.dma("sp", relb.t[:, :], rel_bias[:, :], d_const, writes=[relb])
    c.dma("sp", grev.t[:, :], k_grev[:, :], d_const, writes=[grev])
    c.dma("sp", valid.t[:, :], k_valid[:, :], d_const, writes=[valid])
    c.dma("sp", tmpf[0].t[:, 0:128], k_ident[:, :], d_const, writes=[tmpf[0]])
    c.dma("sp", tmpf[1].t[:, 0:128], k_mask64[:, :], d_const, writes=[tmpf[1]])
    c.dma("sp", bmask.t[:, :], k_bmask[:, :], d_const, writes=[bmask])
    c.dma("sp", tmpf[2].t[0:64, 0:64], k_mask4[:, :], d_const, writes=[tmpf[2]])
    c.dma("sp", gnT.t[:, :], rawap(hgrn_norm, 0, [[1, 128], [128, 8]]), d_const, writes=[gnT], allow_slow_non_contiguous=True)

    c.op("dve", lambda e: e.tensor_scalar(gn125.t[:, :], gn125.t[:, :], 0.125, None, ALU.mult), reads=[gn125], writes=[gn125])
    c.op("act", lambda e: e.activation(esink.t[:, :], esink.t[:, :], AF.Exp), reads=[esink], writes=[esink])
    c.op("pool", lambda e: e.memset(neghalf.t[:, :], -0.5), writes=[neghalf])
    negone = c.sb("negone", [128, 2], F32)
    c.op("pool", lambda e: e.memset(negone.t[:, :], -1.0), writes=[negone])
    c.op("pool", lambda e: e.memset(ones128.t[:, :], 1.0), writes=[ones128])
    c.op("dve", lambda e: e.tensor_tensor(th.t[:, :], lbt.t[:, 0, :], lbt.t[:, 1, :], ALU.subtract), reads=[lbt], writes=[th])
    c.op("act", lambda e: e.activation(th.t[:, :], th.t[:, :], AF.Tanh, scale=0.5), reads=[th], writes=[th])
    c.op("dve", lambda e: e.tensor_scalar(c1.t[:, :], th.t[:, :], -0.25, 0.25, ALU.mult, ALU.add), reads=[th], writes=[c1])
    c.op("dve", lambda e: e.tensor_scalar(c0.t[:, :], th.t[:, :], 0.25, 0.75, ALU.mult, ALU.add), reads=[th], writes=[c0])
    c.op("dve", lambda e: e.tensor_scalar(nc1.t[:, :], th.t[:, :], 0.25, -0.25, ALU.mult, ALU.add), reads=[th], writes=[nc1])
    c.op("dve", lambda e: e.tensor_copy(ident.t[:, :], tmpf[0].t[:, 0:128]), reads=[tmpf[0]], writes=[ident])
    c.op("dve", lambda e: e.tensor_copy(mask64.t[:, :], tmpf[1].t[:, 0:128]), reads=[tmpf[1]], writes=[mask64])
    c.op("dve", lambda e: e.tensor_copy(mask4.t[:, :], tmpf[2].t[0:64, 0:64]), reads=[tmpf[2]], writes=[mask4])
    c.op("dve", lambda e: e.tensor_scalar(gnT.t[:, :], gnT.t[:, :], 0.125, None, ALU.mult), reads=[gnT], writes=[gnT])
    c.op("dve", lambda e: e.tensor_copy(esinkP2.t[0:64, :], esink.t[0:64, 0:16:2]), reads=[esink], writes=[esinkP2])
    c.op("dve", lambda e: e.tensor_copy(esinkP2.t[64:128, :], esink.t[64:128, 1:16:2]), reads=[esink], writes=[esinkP2])

    nw = c.sb("nw", [128, 16], F32)

    def rsqrt(dst, src, R, n, tiles):
        if not c.redirect:
            c.op("pool", lambda e: e.tensor_tensor(dst, src, neghalf.t[R, 0:n], ALU.pow), reads=tiles + [neghalf], writes=tiles)
            return
        y = nw.t[R, 0:n]; t = nw.t[R, 8:8 + n]
        c.op("dve", lambda e: e.tensor_scalar(y.bitcast(I32), src.bitcast(I32), -0.5, 1597463007.0, ALU.mult, ALU.add),
             reads=tiles, writes=[nw])
        for it in range(3):
            c.op("dve", lambda e: e.tensor_tensor(t, y, y, ALU.mult), reads=[nw], writes=[nw])
            c.op("dve", lambda e: e.tensor_tensor(t, t, src, ALU.mult), reads=[nw] + tiles, writes=[nw])
            c.op("dve", lambda e: e.tensor_scalar(t, t, -0.5, 1.5, ALU.mult, ALU.add), reads=[nw], writes=[nw])
            if it < 2:
                c.op("dve", lambda e: e.tensor_tensor(y, y, t, ALU.mult), reads=[nw], writes=[nw])
            else:
                c.op("dve", lambda e: e.tensor_tensor(dst, y, t, ALU.mult), reads=[nw], writes=tiles)

    def late_setup_a():
        pb_ = P()
        c.op("pe", lambda e: e.matmul(pb_.t[0:16, 0:384], relb.t[:, :], grev.t[:, :], start=True, stop=True),
             reads=[relb, grev], writes=[pb_])
        c.op("act", lambda e: e.activation(ev.t[:, :], pb_.t[0:16, 0:384], AF.Exp), reads=[pb_], writes=[ev])
        c.op("dve", lambda e: e.tensor_tensor(ev.t[:, :], ev.t[:, :], valid.t[:, :], ALU.mult), reads=[ev, valid], writes=[ev])
        d_e = c.dsem("escr")
        c.dma("sp", escr.t[:, :], ev.t[:, :], d_e, reads=[ev], writes=[escr])

    def late_setup():
        if late['done']:
            return
        late['done'] = True
        d_rr = c.dsem("rrep")
        for b_ in range(16):
            c.dma("sp", Rrep.t[b_ * 4:(b_ + 1) * 4, :, :], rawap(escr.t, 252, [[1, 4], [384, 16], [1, 4]]), d_rr, reads=[escr], writes=[Rrep])
        for gq in range(4):
            Rt = f1k[gq % 2]
            c.dma("sp", Rt.t[:, :].rearrange("p (h t) -> p h t", h=4), rawap(escr.t, gq * 4 * 384, [[1, 128], [384, 4], [1, 256]]),
                  dyo[gq % 2], reads=[escr], writes=[Rt])
            for hh in range(4):
                src = rawap(Rt.t, hh * 256 + 255, [[1024, 128], [-1, 256]])
                c.op("dve", lambda e, hh=hh, src=src: e.tensor_copy(Et.t[:, gq * 4 + hh, :], src), reads=[Rt], writes=[Et])


    late = {'done': False}

    wk2 = c.sb("wk2", [128, 8, 2, 2, 64], BF16)
    wbv = c.sb("wbv", [128, 8, 128], BF16)
    wple = c.sb("wple", [128, 2, 1024], BF16)
    d_wres = c.dsem("wres")
    d_wple = c.dsem("wple")
    kvsrc = wb_grp[grp_of(C_BK)]
    assert grp_of(C_BV + 127) == grp_of(C_BK)
    resident = {"kv": False, "ple": False}

    def need_kv():
        if resident["kv"]:
            return
        resident["kv"] = True
        for dup in range(2):
            for g in range(2):
                c.dma("sp", wk2.t[:, :, g, dup, :],
                      wb_all[:, C_BK + g * 64:C_BK + g * 64 + 64].rearrange("(k p) d -> p k d", p=128),
                      d_wres, reads=[kvsrc], writes=[wk2])
        c.dma("sp", wbv.t[:, :, :], wb_all[:, C_BV:C_BV + 128].rearrange("(k p) n -> p k n", p=128), d_wres, reads=[kvsrc], writes=[wbv])

    def need_ple():
        if resident["ple"]:
            return
        resident["ple"] = True
        c.dma("sp", wple.t[:, :, :], wb_ple.t[:, :].rearrange("(k p) n -> p k n", p=128), d_wple, reads=[wb_ple], writes=[wple])

    NW = 5
    wbufs = [c.sb("wbuf%d" % i, [128, 8, 512], BF16) for i in range(NW)]
    wds = [c.dsem("wbuf%d" % i) for i in range(NW)]
    wstate = {"i": 0}

    wcache = {}

    def load_w(src_t, col0, extra=(), preload=False):
        key = (src_t.name, col0)
        if not preload and key in wcache:
            return wcache.pop(key)
        i = wstate["i"] % NW
        wstate["i"] += 1
        c.dma("sp", wbufs[i].t[:, :, :], src_t.t[:, col0:col0 + 512].rearrange("(k p) n -> p k n", p=128),
              wds[i], reads=[src_t] + list(extra), writes=[wbufs[i]])
        if preload:
            wcache[key] = wbufs[i]
        return wbufs[i]

    def load_win(col, preload=False):
        a, b = grp_of(col), grp_of(col + 511)
        return load_w(wb_grp[a], col, extra=([wb_grp[b]] if b != a else []), preload=preload)

    big = [c.sb("big%d" % i, [128, 4096], BF16) for i in range(7)]

    def v3(t, a):
        return t.t[:, :].rearrange("p (a b) -> p a b", a=a)

    uT, oaT = big[0], big[1]
    qt = bqT = mT = big[2]
    kt = zs2 = x1T = big[3]
    ktok = obT = big[4]
    vv = big[5]
    gate = big[6]

    xs = [c.sb("xs%d" % i, [128, 1024], F32) for i in range(4)]
    dxs = [c.dsem("xs%d" % i) for i in range(4)]
    preloaded = set()
    for blk_ in range(4):
        c.dma("sp", xs[blk_].t[:, :], xp[blk_ * 128:(blk_ + 1) * 128, :], dxs[blk_], writes=[xs[blk_]])
        preloaded.add(blk_)
    pf = [c.sb("pf%d" % i, [128, 256], F32) for i in range(2)]
    dpf = [c.dsem("pf%d" % i) for i in range(2)]
    pbt = [c.sb("pbt%d" % i, [128, 256], BF16) for i in range(2)]
    pT = c.sb("pT", [128, 2, 512], BF16)
    st4 = [c.sb("st4_%d" % i, [128, 16], F32) for i in range(4)]
    Pl = c.sb("Pl", [128, 8, 8], F32)
    S = c.sb("S", [128, 8, 128], F32)
    S2 = [T(S.t, "S_lo"), T(S.t, "S_hi")]
    Sbf = [c.sb("Sbf%d" % i, [128, 8, 128], BF16) for i in range(3)]
    sstate = {"i": 0}
    attT = [c.sb("attT%d" % i, [128, 8, 128], BF16) for i in range(2)]
    kT2 = c.sb("kT2", [128, 2, 640], BF16)
    vaug = c.sb("vaug", [128, 5, 2, 65], BF16)
    PTt = [c.sb("PTt%d" % i, [128, 512], BF16) for i in range(4)]
    ptstate = {"i": 0}
    den = c.sb("den", [128, 16], F32); rden = c.sb("rden", [128, 16], F32)
    d_misc = c.dsem("misc")

    c.op("dve", lambda e: e.memset(S.t[:, :, :], 0.0), writes=S2)
    c.op("pool", lambda e: e.memset(Sbf[0].t[:, :, :], 0.0), writes=[Sbf[0]])
    c.op("pool", lambda e: e.memset(vaug.t[:, :, :, :], 1.0), writes=[vaug])

    def transposes(src, npart, dst_fn, evac="act"):
        b = P()
        bv = bfv(b)
        for k in range(8):
            c.op("pe", lambda e, k=k: e.transpose(bv[:, k * npart:(k + 1) * npart] if False else bv[:, k * 128:k * 128 + npart],
                                                  src.t[0:npart, k * 128:(k + 1) * 128], ident.t[0:npart, 0:npart]),
                 reads=[src, ident], writes=[b], signal=(k == 7))
        return b, bv

    def mm8(bank_ap, bank, lhs_fn, rhs_fn, reads, nk=8):
        for k in range(nk):
            c.op("pe", lambda e, k=k: e.matmul(bank_ap, lhs_fn(k), rhs_fn(k), start=(k == 0), stop=(k == nk - 1)),
                 reads=reads, writes=[bank], signal=(k == nk - 1))

    def fm(t, ntok):
        return t.t[:, 0:8 * ntok].rearrange("p (k t) -> p k t", k=8)

    def phase0(ntok, bp, x_dram, p_dram, row0):
        for _ in phase0_gen(ntok, bp, x_dram, p_dram, row0):
            pass

    def phase0_gen(ntok, bp, x_dram, p_dram, row0):
        uTv = fm(uT, ntok)
        R = slice(0, bp)
        for blk in range(ntok // bp):
            r0 = row0 + blk * bp
            tcols = slice(blk * bp, (blk + 1) * bp)
            x_ = xs[blk]
            if x_dram is xp and row0 == 0 and blk in preloaded:
                preloaded.discard(blk)
            else:
                c.dma("sp", x_.t[R, :], x_dram[r0:r0 + bp, :], dxs[blk], writes=[x_])
            st = st4[blk]
            junk = B1K()
            c.op("act", lambda e: e.activation(junk.t[R, :], x_.t[R, :], AF.Square, accum_out=st.t[R, 0:1]),
                 reads=[x_], writes=[junk, st], nofold=True)
            c.op("dve", lambda e: e.tensor_scalar(st.t[R, 1:2], st.t[R, 0:1], 1.0 / 1024, EPS, ALU.mult, ALU.add),
                 reads=[st], writes=[st])
            rsqrt(st.t[R, 2:3], st.t[R, 1:2], R, 1, [st])
            ub = B1K()
            c.op("dve", lambda e: e.scalar_tensor_tensor(ub.t[R, :], x_.t[R, :], st.t[R, 2:3], gpre.t[R, :], ALU.mult, ALU.mult),
                 reads=[x_, st, gpre], writes=[ub])
            b, bv = transposes(ub, bp, None)
            c.op("act", lambda e: e.activation(uTv[:, :, tcols],
                                               bv[:, :].rearrange("p (k t) -> p k t", k=8)[:, :, 0:bp], AF.Copy),
                 reads=[b], writes=[uT])
            pf_ = pf[blk % 2]; pb_ = pbt[blk % 2]
            c.dma("sp", pf_.t[R, :], p_dram[r0:r0 + bp, :], dpf[blk % 2], writes=[pf_])
            c.op("pool", lambda e: e.tensor_copy(pb_.t[R, :], pf_.t[R, :]), reads=[pf_], writes=[pb_])
            b2 = P(); bv2 = bfv(b2)
            for k in range(2):
                c.op("pe", lambda e, k=k: e.transpose(bv2[:, k * 128:k * 128 + bp], pb_.t[R, k * 128:(k + 1) * 128], ident.t[R, R]),
                     reads=[pb_, ident], writes=[b2], signal=(k == 1))
            c.op("dve", lambda e: e.tensor_copy(pT.t[:, :, tcols],
                                                bv2[:, 0:256].rearrange("p (k t) -> p k t", k=2)[:, :, 0:bp]),
                 reads=[b2], writes=[pT])
            yield

    def tokmajor_v_gate(ntok, bp, side=None):
        uTv = fm(uT, ntok)
        R = slice(0, bp)
        nblk = ntok // bp
        v3v = v3(vv, 4); g3 = v3(gate, 4)
        for cc in range(2):
            wv = load_win(C_AI + cc * 512)
            for blk in range(nblk):
                tcols = slice(blk * bp, (blk + 1) * bp)
                pv = P()
                mm8(pv.t[R, :], pv, lambda k: uTv[:, k, tcols], lambda k: wv.t[:, k, :], [wv, uT])
                c.op("act", lambda e: e.activation(v3v[R, blk, cc * 512:(cc + 1) * 512], pv.t[R, :], AF.Copy), reads=[pv], writes=[vv])
                if side is not None:
                    next(side, None)
        if side is not None:
            for _ in side:
                pass
        if bp == 64:
            return
        for cc in range(2):
            wg = load_win(C_AOG + cc * 512)
            wz = load_win(C_AZ + cc * 512)
            for blk in range(nblk):
                tcols = slice(blk * bp, (blk + 1) * bp)
                pg = P(); pz = P()
                mm8(pg.t[R, :], pg, lambda k: uTv[:, k, tcols], lambda k: wg.t[:, k, :], [wg, uT])
                mm8(pz.t[R, :], pz, lambda k: uTv[:, k, tcols], lambda k: wz.t[:, k, :], [wz, uT])
                t1 = TF(); t2 = TF(); B1 = TF()
                c.op("act", lambda e: e.activation(t1.t[R, :], pg.t[R, :], AF.Tanh, scale=0.5), reads=[pg], writes=[t1])
                c.op("act", lambda e: e.activation(t2.t[R, :], pz.t[R, :], AF.Tanh, scale=0.5), reads=[pz], writes=[t2])
                c.op("dve", lambda e: e.scalar_tensor_tensor(B1.t[R, :], t2.t[R, :], 1.0, pz.t[R, :], ALU.add, ALU.mult),
                     reads=[t2, pz], writes=[B1])
                c.op("dve", lambda e: e.scalar_tensor_tensor(g3[R, blk, cc * 512:(cc + 1) * 512], t1.t[R, :], 1.0, B1.t[R, :], ALU.add, ALU.mult),
                     reads=[t1, B1], writes=[gate])

    def phase4(ntok, bp, y_dram, row0, side=None, after_wg=None):
        uTv = fm(uT, ntok); oaTv = fm(oaT, ntok); obTv = fm(obT, ntok); mTv = fm(mT, ntok); x1Tv = fm(x1T, ntok)
        R = slice(0, bp)
        nblk = ntok // bp
        N = slice(0, ntok)
        for cc in range(2):
            wpa_ = load_w(wb_ab, cc * 512); wpb_ = load_w(wb_ab, 1024 + cc * 512)
            wga = load_win(C_GA + cc * 512); wgb = load_win(C_GB + cc * 512)
            for g4 in range(4):
                gs = slice(g4 * 128, g4 * 128 + 128)
                pa = P(); pb = P(); pga = P(); pgb = P()
                mm8(pa.t[:, N], pa, lambda k: wpa_.t[:, k, gs], lambda k: oaTv[:, k, :], [wpa_, oaT])
                mm8(pb.t[:, N], pb, lambda k: wpb_.t[:, k, gs], lambda k: obTv[:, k, :], [wpb_, obT])
                mm8(pga.t[:, N], pga, lambda k: wga.t[:, k, gs], lambda k: uTv[:, k, :], [wga, uT])
                mm8(pgb.t[:, N], pgb, lambda k: wgb.t[:, k, gs], lambda k: uTv[:, k, :], [wgb, uT])
                ta = TF(); tb = TF(); m1 = TF(); m2 = TF()
                c.op("act", lambda e: e.activation(ta.t[:, N], pga.t[:, N], AF.Tanh, scale=0.5), reads=[pga], writes=[ta])
                c.op("act", lambda e: e.activation(tb.t[:, N], pgb.t[:, N], AF.Tanh, scale=0.5), reads=[pgb], writes=[tb])
                c.op("dve", lambda e: e.scalar_tensor_tensor(m1.t[:, N], ta.t[:, N], 1.0, pa.t[:, N], ALU.add, ALU.mult), reads=[ta, pa], writes=[m1])
                c.op("dve", lambda e: e.scalar_tensor_tensor(m2.t[:, N], tb.t[:, N], 1.0, pb.t[:, N], ALU.add, ALU.mult), reads=[tb, pb], writes=[m2])
                c.op("pool", lambda e: e.tensor_tensor(mTv[:, cc * 4 + g4, :], m1.t[:, N], m2.t[:, N], ALU.add), reads=[m1, m2], writes=[mT])
        mark('p4.y')
        wo = [load_w(wb_og, 0), load_w(wb_og, 512)]

        def emit_x1T(x1b, cols):
            b, bv = transposes(x1b, bp, None)
            c.op("act", lambda e: e.activation(x1Tv[:, :, cols], bv[:, :].rearrange("p (k t) -> p k t", k=8)[:, :, 0:bp], AF.Copy),
                 reads=[b], writes=[x1T])

        pend_x1 = None
        for blk in range(nblk):
            cols = slice(blk * bp, (blk + 1) * bp)
            x_ = xs[blk]; st = st4[blk]
            py = [P(), P()]
            for cc in range(2):
                mm8(py[cc].t[R, :], py[cc], lambda k: mTv[:, k, cols], lambda k: wo[cc].t[:, k, :], [mT, wo[cc]])
                junk = b1k[0]
                c.op("act", lambda e, cc=cc: e.activation(junk.t[R, 0:512], py[cc].t[R, :], AF.Square, accum_out=st.t[R, cc:cc + 1]),
                     reads=[py[cc]], writes=[junk, st], nofold=True)
            c.op("dve", lambda e: e.tensor_tensor(st.t[R, 2:3], st.t[R, 0:1], st.t[R, 1:2], ALU.add), reads=[st], writes=[st])
            c.op("dve", lambda e: e.tensor_scalar(st.t[R, 3:4], st.t[R, 2:3], 0.25 / 1024, EPS, ALU.mult, ALU.add), reads=[st], writes=[st])
            rsqrt(st.t[R, 4:5], st.t[R, 3:4], R, 1, [st])
            c.op("dve", lambda e: e.tensor_scalar(st.t[R, 5:6], st.t[R, 4:5], 0.5, None, ALU.mult), reads=[st], writes=[st])
            for cc in range(2):
                tt = TF()
                c.op("dve", lambda e, cc=cc: e.scalar_tensor_tensor(tt.t[R, :], py[cc].t[R, :], st.t[R, 5:6], gpost.t[R, cc * 512:(cc + 1) * 512], ALU.mult, ALU.mult),
                     reads=[py[cc], st, gpost], writes=[tt])
                c.op("pool", lambda e, cc=cc: e.tensor_tensor(x_.t[R, cc * 512:(cc + 1) * 512], tt.t[R, :], x_.t[R, cc * 512:(cc + 1) * 512], ALU.add),
                     reads=[tt, x_], writes=[x_])
            x1b = b1k[1 + blk % 2]
            c.op("act", lambda e: e.activation(x1b.t[R, :], x_.t[R, :], AF.Copy), reads=[x_], writes=[x1b])
            if pend_x1 is not None:
                emit_x1T(*pend_x1)
            pend_x1 = (x1b, cols)
        emit_x1T(*pend_x1)
        mark('p4.ple')
        wg = [load_w(wb_og, 1024), load_w(wb_og, 1536)]
        if after_wg is not None:
            after_wg()
        need_ple()
        for blk in range(nblk):
            cols = slice(blk * bp, (blk + 1) * bp)
            r0 = row0 + blk * bp
            x_ = xs[blk]
            yo_ = yo[blk % 2]
            for cc in range(2):
                pg = P(); pe_ = P()
                mm8(pg.t[R, :], pg, lambda k: x1Tv[:, k, cols], lambda k: wg[cc].t[:, k, :], [x1T, wg[cc]])
                mm8(pe_.t[R, :], pe_, lambda k: pT.t[:, k, cols], lambda k: wple.t[:, k, cc * 512:(cc + 1) * 512], [pT, wple], nk=2)
                tg = TF(); e2_ = TF()
                c.op("act", lambda e: e.activation(tg.t[R, :], pg.t[R, :], AF.Tanh, scale=0.5), reads=[pg], writes=[tg])
                c.op("dve", lambda e: e.scalar_tensor_tensor(e2_.t[R, :], tg.t[R, :], 1.0, pe_.t[R, :], ALU.add, ALU.mult), reads=[tg, pe_], writes=[e2_])
                c.op("dve", lambda e, cc=cc: e.scalar_tensor_tensor(yo_.t[R, cc * 512:(cc + 1) * 512], e2_.t[R, :], 0.5, x_.t[R, cc * 512:(cc + 1) * 512], ALU.mult, ALU.add),
                     reads=[e2_, x_], writes=[yo_])
            c.dma("act", y_dram[r0:r0 + bp, :], yo_.t[R, :], dyo[blk % 2], reads=[yo_])
            if side is not None and blk >= 1:
                sv = c.redirect; c.redirect = False
                next(side, None)
                c.redirect = sv
        if side is not None:
            sv = c.redirect; c.redirect = False
            for _ in side:
                pass
            c.redirect = sv

    def mark(lbl):
        MARKS.append((lbl, c.nins['pe']))

    def prompt_tile(ti):
        uT3 = v3(uT, 8)
        mark('t%d.p0' % ti)
        phase0(NT, 128, xp, pp, ti * NT)
        cast_stage(1)
        if STOP <= 1:
            return
        mark('t%d.p1' % ti)
        qt3 = v3(qt, 8); kt3 = v3(kt, 8)
        for hg in range(2):
            wq = load_win(C_AQ + hg * 512)
            wf = load_win(C_AF + hg * 512)
            for hh in range(4):
                h = hg * 4 + hh
                pq = P(); pf2 = P()
                mm8(pq.t[:, :], pq, lambda k: wq.t[:, k, hh * 128:(hh + 1) * 128], lambda k: uT3[:, k, :], [wq, uT])
                mm8(pf2.t[:, :], pf2, lambda k: wf.t[:, k, hh * 128:(hh + 1) * 128], lambda k: uT3[:, k, :], [wf, uT])
                tq = TF(); A = TF(); tf = TF(); fg = TF(); Pc = TF(); kk = TF()
                c.op("act", lambda e: e.activation(tq.t[:, :], pq.t[:, :], AF.Tanh, scale=0.5), reads=[pq], writes=[tq])
                c.op("dve", lambda e: e.scalar_tensor_tensor(A.t[:, :], tq.t[:, :], 1.0, pq.t[:, :], ALU.add, ALU.mult),
                     reads=[tq, pq], writes=[A])
                c.op("act", lambda e: e.activation(tf.t[:, :], pf2.t[:, :], AF.Tanh, scale=0.5), reads=[pf2], writes=[tf])
                c.op("act", lambda e: e.activation(fg.t[:, :], tf.t[:, :], AF.Identity, scale=c1.t[:, h:h + 1], bias=c0.t[:, h:h + 1]),
                     reads=[tf, c1, c0], writes=[fg])
                c.op("dve", lambda e: e.tensor_tensor_scan(Pc.t[:, :], scanm.t[:, :], fg.t[:, :], 1.0, ALU.max, ALU.mult),
                     reads=[scanm, fg], writes=[Pc])
                pe1 = "dve" if ti == 0 else "pool"
                c.op(pe1, lambda e: e.tensor_tensor(qt3[:, h, :], A.t[:, :], Pc.t[:, :], ALU.mult),
                     reads=[A, Pc], writes=[qt])
                c.op("act", lambda e: e.activation(Pl.t[:, h, :], Pc.t[:, :].rearrange("p (c t) -> p c t", t=64)[:, :, 63], AF.Copy),
                     reads=[Pc], writes=[Pl])
                c.op("act", lambda e: e.activation(kk.t[:, :], tf.t[:, :], AF.Identity, scale=nc1.t[:, h:h + 1], bias=c1.t[:, h:h + 1]),
                     reads=[tf, nc1, c1], writes=[kk])
                if USE_POOL_DIV:
                    rP = TF()
                    c.op("pool", lambda e: e.tensor_tensor(rP.t[:, :], Pc.t[:, :], bc(negone.t[:, 0], 512), ALU.pow), reads=[Pc, negone], writes=[rP])
                    c.op(pe1, lambda e: e.tensor_tensor(kt3[:, h, :], kk.t[:, :], rP.t[:, :], ALU.mult),
                         reads=[kk, rP], writes=[kt])
                else:
                    rP = TF()
                    c.op("dve", lambda e: e.reciprocal(rP.t[:, :], Pc.t[:, :]), reads=[Pc], writes=[rP])
                    c.op(pe1, lambda e: e.tensor_tensor(kt3[:, h, :], kk.t[:, :], rP.t[:, :], ALU.mult),
                         reads=[kk, rP], writes=[kt])
        mark('t%d.ktok' % ti)
        ktok3 = v3(ktok, 4)

        def ktok_gen():
            for blk in range(4):
                b = P(); bv = bfv(b)
                for h in range(8):
                    c.op("pe", lambda e, h=h: e.transpose(bv[:, h * 128:(h + 1) * 128], kt3[:, h, blk * 128:(blk + 1) * 128], ident.t[:, :]),
                         reads=[kt, ident], writes=[b], signal=(h == 7))
                c.op("act", lambda e: e.activation(ktok3[:, blk, :], bv[:, :], AF.Copy), reads=[b], writes=[ktok])
                yield

        late_setup()
        mark('t%d.vgate' % ti)
        cast_stage(2)
        v3v = v3(vv, 4); g3 = v3(gate, 4)
        tokmajor_v_gate(NT, 128, side=ktok_gen())
        cast_stage(3)
        if STOP <= 2:
            return
        mark('t%d.p2' % ti)
        oaT3 = v3(oaT, 8)

        def s_mm(blk, ci):
            rows = slice(ci * 64, ci * 64 + 64)
            out = []
            for half in range(2):
                ps_ = P()
                for hh in range(4):
                    h = half * 4 + hh
                    c.op("pe", lambda e, h=h, hh=hh: e.matmul(ps_.t[:, hh * 128:(hh + 1) * 128],
                                                             ktok3[rows, blk, h * 128:(h + 1) * 128],
                                                             v3v[rows, blk, h * 128:(h + 1) * 128], start=True, stop=True),
                         reads=[ktok, vv], writes=[ps_], signal=(hh == 3))
                out.append(ps_)
            return out

        def s_chain(blk, ci, pss):
            ch = blk * 2 + ci
            s_old = Sbf[sstate["i"] % 3]
            sstate["i"] += 1
            s_new = Sbf[sstate["i"] % 3]
            for half in range(2):
                ps_ = pss[half]
                hs = slice(half * 4, half * 4 + 4)
                c.op("dve", lambda e: e.tensor_tensor(S.t[:, hs, :], ps_.t[:, :].rearrange("p (h d) -> p h d", h=4), S.t[:, hs, :], ALU.add),
                     reads=[ps_, S2[half]], writes=[S2[half]])
                c.op("dve", lambda e: e.tensor_tensor(S.t[:, hs, :], S.t[:, hs, :], bc(Pl.t[:, hs, ch], 128), ALU.mult),
                     reads=[S2[half], Pl], writes=[S2[half]])
                c.op("act", lambda e: e.activation(s_new.t[:, hs, :], S.t[:, hs, :], AF.Copy), reads=[S2[half]], writes=[s_new])
            return s_old, s_new

        def emit_oaT(oa, cols):
            b, bv = transposes(oa, 128, None)
            c.op("act", lambda e: e.activation(oaT3[:, :, cols], bv[:, :].rearrange("p (k t) -> p k t", k=8), AF.Copy),
                 reads=[b], writes=[oaT])

        bstate["list"] = [0, 1, 2, 3, 4, 7]
        pend_oa = None
        for blk in range(4):
            cols = slice(blk * 128, blk * 128 + 128)
            g2 = b1k[0]
            c.op("pool", lambda e: e.tensor_tensor(g2.t[:, :], g3[:, blk, :], gn125.t[:, :], ALU.mult),
                 reads=[gate, gn125], writes=[g2])
            aT = attT[blk % 2]
            for half in range(2):
                pa = P()
                for hh in range(4):
                    h = half * 4 + hh
                    c.op("pe", lambda e, h=h, hh=hh: e.matmul(pa.t[:, hh * 128:(hh + 1) * 128], kt3[:, h, cols], qt3[:, h, cols],
                                                             start=True, stop=True),
                         reads=[kt, qt], writes=[pa], signal=(hh == 3))
                c.op("dve", lambda e: e.tensor_tensor(aT.t[:, half * 4:half * 4 + 4, :],
                                                      pa.t[:, :].rearrange("p (h t) -> p h t", h=4),
                                                      bcmid(mask64.t[:, :], 4), ALU.mult),
                     reads=[pa, mask64], writes=[aT])
            pssA = s_mm(blk, 0)
            pssB = s_mm(blk, 1)
            s0, s1 = s_chain(blk, 0, pssA)
            s_chain(blk, 1, pssB)
            po = [banks[5], banks[6]]
            for h in range(8):
                pb = po[h // 4]
                oc = slice((h % 4) * 128, (h % 4) * 128 + 128)
                c.op("pe", lambda e, h=h: e.matmul(pb.t[:, oc], aT.t[:, h, :], v3v[:, blk, h * 128:(h + 1) * 128], start=True, stop=False),
                     reads=[aT, vv], writes=[pb], signal=False)
                c.op("pe", lambda e, h=h: e.matmul(pb.t[0:64, oc], qt3[:, h, blk * 128:blk * 128 + 64], s0.t[:, h, :], start=False, stop=True),
                     reads=[qt, s0], writes=[pb], signal=False)
                c.op("pe", lambda e, h=h: e.matmul(pb.t[64:128, oc], qt3[:, h, blk * 128 + 64:blk * 128 + 128], s1.t[:, h, :], start=False, stop=True),
                     reads=[qt, s1], writes=[pb], signal=(h % 4 == 3))
            sq = F1K(); st = st4[blk]
            for half in range(2):
                c.op("act", lambda e, half=half: e.activation(sq.t[:, half * 512:(half + 1) * 512], po[half].t[:, :], AF.Square),
                     reads=[po[half]], writes=[sq])
            c.op("dve", lambda e: e.tensor_reduce(st.t[:, 0:8], sq.t[:, :].rearrange("p (h d) -> p h d", h=8), AX.X, ALU.add),
                 reads=[sq], writes=[st])
            c.op("dve", lambda e: e.tensor_scalar(st.t[:, 0:8], st.t[:, 0:8], 0.25 / 128, EPS, ALU.mult, ALU.add), reads=[st], writes=[st])
            rsqrt(st.t[:, 8:16], st.t[:, 0:8], slice(0, 128), 8, [st])
            ot = F1K()
            for half in range(2):
                c.op("dve", lambda e, half=half: e.tensor_tensor(ot.t[:, half * 512:(half + 1) * 512].rearrange("p (h d) -> p h d", h=4),
                                                                po[half].t[:, :].rearrange("p (h d) -> p h d", h=4),
                                                                bc(st.t[:, 8 + half * 4:12 + half * 4], 128), ALU.mult),
                     reads=[po[half], st], writes=[ot])
            oa = b1k[1 + blk % 2]
            c.op("pool", lambda e: e.tensor_tensor(oa.t[:, :], ot.t[:, :], g2.t[:, :], ALU.mult), reads=[ot, g2], writes=[oa])
            if pend_oa is not None:
                emit_oaT(*pend_oa)
            pend_oa = (oa, cols)
        emit_oaT(*pend_oa)
        bstate["list"] = None

        if STOP <= 3:
            return
        mark('t%d.p3in' % ti)
        c.redirect = False
        need_kv()
        bq3 = v3(bqT, 8)
        for cc in range(2):
            wq = load_win(C_BQ + cc * 512)
            for g4 in range(4):
                pq = P()
                mm8(pq.t[:, :], pq, lambda k: wq.t[:, k, g4 * 128:(g4 + 1) * 128], lambda k: uT3[:, k, :], [wq, uT])
                eng = "act" if g4 % 2 == 0 else "dve"
                if eng == "act":
                    c.op("act", lambda e: e.activation(bq3[:, cc * 4 + g4, :], pq.t[:, :], AF.Copy), reads=[pq], writes=[bqT])
                else:
                    c.op("dve", lambda e: e.tensor_copy(bq3[:, cc * 4 + g4, :], pq.t[:, :]), reads=[pq], writes=[bqT])
        if ti > 0:
            c.op("pool", lambda e: e.tensor_copy(kT2.t[:, :, 0:128], kT2.t[:, :, 512:640]), reads=[kT2], writes=[kT2])
            c.op("pool", lambda e: e.tensor_copy(vaug.t[:, 0, :, :], vaug.t[:, 4, :, :]), reads=[vaug], writes=[vaug])
        for g in range(2):
            pk = P()
            mm8(pk.t[:, :], pk, lambda k: wk2.t[:, k, g, :, :], lambda k: uT3[:, k, :], [wk2, uT])
            c.op("dve", lambda e: e.tensor_copy(kT2.t[:, g, 128:640], pk.t[:, :]), reads=[pk], writes=[kT2])
        for blk in range(4):
            pv = P()
            mm8(pv.t[:, 0:128], pv, lambda k: uT3[:, k, blk * 128:(blk + 1) * 128], lambda k: wbv.t[:, k, :], [wbv, uT])
            c.op("act", lambda e: e.activation(vaug.t[:, blk + 1, :, 0:64], pv.t[:, 0:128].rearrange("p (g d) -> p g d", g=2), AF.Copy),
                 reads=[pv], writes=[vaug])
        zs3 = v3(zs2, 4)
        for cc in range(2):
            wz = load_win(C_BZ + cc * 512)
            for blk in range(4):
                pz = P()
                mm8(pz.t[:, :], pz, lambda k: uT3[:, k, blk * 128:(blk + 1) * 128], lambda k: wz.t[:, k, :], [wz, uT])
                t2 = TF()
                c.op("act", lambda e: e.activation(t2.t[:, :], pz.t[:, :], AF.Tanh, scale=0.5), reads=[pz], writes=[t2])
                c.op("dve", lambda e: e.scalar_tensor_tensor(zs3[:, blk, cc * 512:(cc + 1) * 512], t2.t[:, :], 1.0, pz.t[:, :], ALU.add, ALU.mult),
                     reads=[t2, pz], writes=[zs2])
        obT3 = v3(obT, 8)
        mark('t%d.p3att' % ti)

        def emit_obT(ob, cols_):
            b, bv = transposes(ob, 128, None)
            c.op("act", lambda e: e.activation(obT3[:, :, cols_], bv[:, :].rearrange("p (k t) -> p k t", k=8), AF.Copy),
                 reads=[b], writes=[obT])

        pend_ob = [None]
        for blk in range(4):
            gb = ti * 4 + blk
            cols = slice(blk * 128, blk * 128 + 128)
            pvb = [banks[5], banks[6], banks[7]]
            def scores(pgp):
                pss = [P(), P()]
                nmm = 2 if gb == 0 else 4
                for e2 in range(2):
                    rows = slice(e2 * 64, e2 * 64 + 64)
                    ps_ = pss[e2]
                    i = 0
                    for j in range(2):
                        hq = 4 * pgp + 2 * j + e2; g = hq // 8; pair = hq // 2
                        i += 1
                        c.op("pe", lambda e, rows=rows, g=g, j=j, pair=pair, ps_=ps_: e.matmul(
                            ps_.t[:, j * 256:j * 256 + 128], kT2.t[rows, g, 128 + blk * 128:256 + blk * 128],
                            bq3[rows, pair, cols], start=True, stop=True),
                             reads=[kT2, bqT], writes=[ps_], signal=(i == nmm))
                        if gb > 0:
                            i += 1
                            c.op("pe", lambda e, rows=rows, g=g, j=j, pair=pair, ps_=ps_: e.matmul(
                                ps_.t[:, j * 256 + 128:j * 256 + 256], kT2.t[rows, g, blk * 128:128 + blk * 128],
                                bq3[rows, pair, cols], start=True, stop=True),
                                 reads=[kT2, bqT], writes=[ps_], signal=(i == nmm))
                return pss

            def softmax_num(pgp, pss):
                pts = []
                for e2 in range(2):
                    ps_ = pss[e2]
                    eP = TF()
                    pt = PTt[ptstate["i"] % 4]; ptstate["i"] += 1
                    h0 = 4 * pgp + e2
                    Esl = Et.t[:, h0:h0 + 3:2, :]
                    if gb > 0:
                        c.op("act", lambda e: e.activation(eP.t[:, :], ps_.t[:, :], AF.Exp, scale=0.125), reads=[ps_], writes=[eP])
                        eng = "dve"
                        c.op(eng, lambda e: e.tensor_tensor(pt.t[:, :].rearrange("p (j x) -> p j x", j=2),
                                                            eP.t[:, :].rearrange("p (j x) -> p j x", j=2), Esl, ALU.mult),
                             reads=[eP, Et], writes=[pt])
                    else:
                        c.op("act", lambda e: e.activation(eP.t[:, :].rearrange("p (h s t) -> p h s t", h=2, s=2)[:, :, 0, :],
                                                           ps_.t[:, :].rearrange("p (h s t) -> p h s t", h=2, s=2)[:, :, 0, :],
                                                           AF.Exp, scale=0.125), reads=[ps_], writes=[eP])
                        c.op("dve", lambda e: e.tensor_tensor(pt.t[:, :].rearrange("p (h s t) -> p h s t", h=2, s=2)[:, :, 0, :],
                                                              eP.t[:, :].rearrange("p (h s t) -> p h s t", h=2, s=2)[:, :, 0, :],
                                                              Esl[:, :, 0:128], ALU.mult),
                             reads=[eP, Et], writes=[pt])
                    pts.append(pt)
                return pts

            def pv_mm(pgp, pts):
                for e2 in range(2):
                    pt = pts[e2]
                    for j in range(2):
                        hq = 4 * pgp + 2 * j + e2; g = hq // 8
                        pb = pvb[hq // 7]; off = (hq % 7) * 65
                        c.op("pe", lambda e, j=j, g=g, off=off, pb=pb, pt=pt: e.matmul(
                            pb.t[:, off:off + 65], pt.t[:, j * 256:j * 256 + 128], vaug.t[:, blk + 1, g, :], start=True, stop=(gb == 0)),
                             reads=[pt, vaug], writes=[pb], signal=(gb == 0))
                        if gb > 0:
                            c.op("pe", lambda e, j=j, g=g, off=off, pb=pb, pt=pt: e.matmul(
                                pb.t[:, off:off + 65], pt.t[:, j * 256 + 128:j * 256 + 256], vaug.t[:, blk, g, :], start=False, stop=True),
                                 reads=[pt, vaug], writes=[pb], signal=True)

            pend = None
            for pgp in range(4):
                pss = scores(pgp)
                if pgp == 2 and pend_ob[0] is not None:
                    emit_obT(*pend_ob[0])
                    pend_ob[0] = None
                if pend is not None:
                    pv_mm(*pend)
                pts = softmax_num(pgp, pss)
                pend = (pgp, pts)
            pv_mm(*pend)
            obt = F1K()
            for j in range(3):
                nh = 7 if j < 2 else 2
                pv3 = pvb[j].t[:, 0:nh * 65].rearrange("p (h d) -> p h d", d=65)
                c.op("dve", lambda e, j=j, nh=nh, pv3=pv3: e.tensor_tensor(den.t[:, 7 * j:7 * j + nh], pv3[:, :, 64], esink.t[:, 7 * j:7 * j + nh], ALU.add),
                     reads=[pvb[j], esink], writes=[den])
            c.op("dve", lambda e: e.tensor_scalar(den.t[:, :], den.t[:, :], 2.0, None, ALU.mult), reads=[den], writes=[den])
            c.op("dve", lambda e: e.reciprocal(rden.t[:, :], den.t[:, :]), reads=[den], writes=[rden])
            for j in range(3):
                nh = 7 if j < 2 else 2
                pv3 = pvb[j].t[:, 0:nh * 65].rearrange("p (h d) -> p h d", d=65)
                c.op("dve", lambda e, j=j, nh=nh, pv3=pv3: e.tensor_tensor(obt.t[:, 7 * j * 64:(7 * j + nh) * 64].rearrange("p (h d) -> p h d", d=64),
                                                                          pv3[:, :, 0:64], bc(rden.t[:, 7 * j:7 * j + nh], 64), ALU.mult),
                     reads=[pvb[j], rden], writes=[obt])
            ob = b1k[1 + blk % 2]
            c.op("pool", lambda e: e.tensor_tensor(ob.t[:, :], obt.t[:, :], zs3[:, blk, :], ALU.mult), reads=[obt, zs2], writes=[ob])
            pend_ob[0] = (ob, cols)
        emit_obT(*pend_ob[0])

        if STOP <= 4:
            return
        mark('t%d.p4' % ti)
        def pre_sample():
            load_win(C_AQ, preload=True); load_win(C_AQ + 512, preload=True); load_win(C_AF, preload=True)

        phase4(NT, 128, yp, ti * NT, side=None, after_wg=(pre_sample if ti == NTL - 1 else None))


    def sample_path():
        late_setup()
        NS = 64
        sx_list = [xs[1], xs[2], xs[3]]
        sx_ds = [dxs[1], dxs[2], dxs[3]]

        def issue_in(bi):
            Sxi = sx_list[bi % 3]
            c.dma("sp", Sxi.t[:, :].rearrange("p (h d) -> p h d", h=8), st0[bi].rearrange("h p v -> p h v"), sx_ds[bi % 3], writes=[Sxi])

        issue_in(0); issue_in(1)
        mark('sample')
        R64 = slice(0, 64)
        phase0(NS, 64, xsm, psm, 0)
        uTs = fm(uT, NS)

        def fm_proj(col0):
            bk_ = P()
            for hg in range(2):
                w = load_win(col0 + hg * 512)
                for hh in range(4):
                    h = hg * 4 + hh
                    mm8(bk_.t[:, h * 64:(h + 1) * 64], bk_, lambda k: w.t[:, k, hh * 128:(hh + 1) * 128], lambda k: uTs[:, k, :], [w, uT])
            return bk_

        def h3(a):
            return a.rearrange("p (h t) -> p h t", h=8)

        Pcs = S.t[:, 0:4, :].rearrange("p a b -> p (a b)")
        gT2 = S.t[:, 4:8, :].rearrange("p a b -> p (a b)")
        qts = qt.t[:, 0:512]; kts = qt.t[:, 512:1024]; khs = qt.t[:, 1024:1536]
        khtok = qt.t[0:64, 2048:3072]
        pq = fm_proj(C_AQ); pfb = fm_proj(C_AF)
        tq = TF(); A = TF(); tf = TF(); fg = TF(); kk = TF(); rP = TF(); m4 = TF()
        c.op("act", lambda e: e.activation(tq.t[:, :], pq.t[:, :], AF.Tanh, scale=0.5), reads=[pq], writes=[tq])
        c.op("dve", lambda e: e.scalar_tensor_tensor(A.t[:, :], tq.t[:, :], 1.0, pq.t[:, :], ALU.add, ALU.mult), reads=[tq, pq], writes=[A])
        c.op("act", lambda e: e.activation(tf.t[:, :], pfb.t[:, :], AF.Tanh, scale=0.5), reads=[pfb], writes=[tf])
        c.op("dve", lambda e: e.tensor_tensor(h3(fg.t[:, :]), h3(tf.t[:, :]), bc(c1.t[:, 0:8], 64), ALU.mult), reads=[tf, c1], writes=[fg])
        c.op("dve", lambda e: e.tensor_tensor(h3(fg.t[:, :]), h3(fg.t[:, :]), bc(c0.t[:, 0:8], 64), ALU.add), reads=[fg, c0], writes=[fg])
        c.op("pool", lambda e: e.memset(m4.t[:, :], 0.0), writes=[m4])
        c.op("pool", lambda e: e.memset(m4.t[:, :].rearrange("p (a t) -> p a t", t=4)[:, :, 0], 1.0), writes=[m4])
        c.op("dve", lambda e: e.tensor_tensor_scan(Pcs, m4.t[:, :], fg.t[:, :], 1.0, ALU.max, ALU.mult), reads=[m4, fg], writes=S2)
        c.op("pool", lambda e: e.tensor_tensor(qts, A.t[:, :], Pcs, ALU.mult), reads=[A] + S2, writes=[qt])
        c.op("pool", lambda e: e.tensor_tensor(h3(kk.t[:, :]), h3(tf.t[:, :]), bc(nc1.t[:, 0:8], 64), ALU.mult), reads=[tf, nc1], writes=[kk])
        c.op("pool", lambda e: e.tensor_tensor(h3(kk.t[:, :]), h3(kk.t[:, :]), bc(c1.t[:, 0:8], 64), ALU.add), reads=[kk, c1], writes=[kk])
        c.op("dve", lambda e: e.reciprocal(rP.t[:, :], Pcs), reads=S2, writes=[rP])
        c.op("pool", lambda e: e.tensor_tensor(kts, kk.t[:, :], rP.t[:, :], ALU.mult), reads=[kk, rP], writes=[qt])
        P4all = bc(Pcs.rearrange("p (a t) -> p a t", t=4)[:, :, 3], 4)
        c.op("dve", lambda e: e.tensor_tensor(khs.rearrange("p (a t) -> p a t", t=4), kts.rearrange("p (a t) -> p a t", t=4), P4all, ALU.mult),
             reads=[qt] + S2, writes=[qt])
        b = P(); bv = bfv(b)
        for h in range(8):
            c.op("pe", lambda e, h=h: e.transpose(bv[0:64, h * 128:(h + 1) * 128], khs[:, h * 64:(h + 1) * 64], ident.t[:, :]),
                 reads=[qt, ident], writes=[b], signal=(h == 7))
        c.op("act", lambda e: e.activation(khtok, bv[0:64, :], AF.Copy), reads=[b], writes=[qt])
        pg = fm_proj(C_AOG); pz = fm_proj(C_AZ)
        t1 = TF(); t2 = TF(); B1 = TF(); gT = TF()
        c.op("act", lambda e: e.activation(t1.t[:, :], pg.t[:, :], AF.Tanh, scale=0.5), reads=[pg], writes=[t1])
        c.op("act", lambda e: e.activation(t2.t[:, :], pz.t[:, :], AF.Tanh, scale=0.5), reads=[pz], writes=[t2])
        c.op("dve", lambda e: e.scalar_tensor_tensor(B1.t[:, :], t2.t[:, :], 1.0, pz.t[:, :], ALU.add, ALU.mult), reads=[t2, pz], writes=[B1])
        c.op("dve", lambda e: e.scalar_tensor_tensor(gT.t[:, :], t1.t[:, :], 1.0, B1.t[:, :], ALU.add, ALU.mult), reads=[t1, B1], writes=[gT])
        c.op("pool", lambda e: e.tensor_tensor(h3(gT2), h3(gT.t[:, :]), bc(gnT.t[:, 0:8], 64), ALU.mult), reads=[gT, gnT], writes=S2)
        tokmajor_v_gate(NS, 64)
        v3v = v3(vv, 4)
        vs_ = v3v[0:64, 0, :]
        pa = P()
        for h in range(8):
            c.op("pe", lambda e, h=h: e.matmul(pa.t[0:64, h * 64:(h + 1) * 64], kts[:, h * 64:(h + 1) * 64], qts[:, h * 64:(h + 1) * 64], start=True, stop=True),
                 reads=[qt], writes=[pa], signal=(h == 7))
        aT = attT[0]
        c.op("pool", lambda e: e.memset(aT.t[64:128, :, :], 0.0), writes=[aT])
        c.op("dve", lambda e: e.tensor_tensor(aT.t[0:64, :, 0:64], pa.t[0:64, :].rearrange("p (h t) -> p h t", h=8),
                                              bcmid(mask4.t[:, :], 8), ALU.mult), reads=[pa, mask4], writes=[aT])
        def swa_gen():
            need_kv()
            bqTs = kt.t[:, 0:512]; kTn = kt.t[:, 512:640]; zsT = kt.t[:, 1024:1536]
            KcT = ktok
            KcT4 = KcT.t[:, :].rearrange("p (b g s) -> p b g s", b=16, g=2)
            vc = gate.t[:, 0:2048].rearrange("p (b f) -> p b f", b=16)
            vnew = gate.t[:, 2048:2176]
            pbq = P()
            for cc in range(2):
                wq = load_win(C_BQ + cc * 512)
                for g4 in range(4):
                    cq = cc * 4 + g4
                    mm8(pbq.t[:, cq * 64:(cq + 1) * 64], pbq, lambda k: wq.t[:, k, g4 * 128:(g4 + 1) * 128], lambda k: uTs[:, k, :], [wq, uT])
            c.op("act", lambda e: e.activation(bqTs, pbq.t[:, :], AF.Copy), reads=[pbq], writes=[kt])
            yield
            pkn = P()
            for g in range(2):
                mm8(pkn.t[:, g * 64:(g + 1) * 64], pkn, lambda k: wk2.t[:, k, g, :, :], lambda k: uTs[:, k, :], [wk2, uT])
            c.op("dve", lambda e: e.tensor_copy(kTn, pkn.t[:, 0:128]), reads=[pkn], writes=[kt])
            yield
            knv = TF()
            pvn = P()
            mm8(pvn.t[0:64, 0:128], pvn, lambda k: uTs[:, k, :], lambda k: wbv.t[:, k, :], [wbv, uT])
            c.op("pool", lambda e: e.memset(vnew, 0.0), writes=[gate])
            c.op("act", lambda e: e.activation(vnew[0:64, :], pvn.t[0:64, 0:128], AF.Copy), reads=[pvn], writes=[gate])
            c.op("dve", lambda e: e.tensor_copy(knv.t[0:64, 0:128], pvn.t[0:64, 0:128]), reads=[pvn], writes=[knv])
            pkk = P()
            for g in range(2):
                mm8(pkk.t[0:64, g * 64:(g + 1) * 64], pkk, lambda k: uTs[:, k, :], lambda k: wk2.t[:, k, g, 0, :], [wk2, uT])
            c.op("dve", lambda e: e.tensor_copy(knv.t[0:64, 128:256], pkk.t[0:64, 0:128]), reads=[pkk], writes=[knv])
            for b_ in range(16):
                c.dma("sp", vs_o[b_, 124:128, :], knv.t[b_ * 4:(b_ + 1) * 4, 0:128], d_misc, reads=[knv])
                c.dma("sp", ks_o[b_, 124:128, :], knv.t[b_ * 4:(b_ + 1) * 4, 128:256], d_misc, reads=[knv])
            c.dma("sp", ks_o[:, 0:124, :], ck[:, 4:128, :], d_misc)
            c.dma("sp", vs_o[:, 0:124, :], cv[:, 4:128, :], d_misc)
            yield
            pbz = P()
            for cc in range(2):
                wz = load_win(C_BZ + cc * 512)
                for g4 in range(4):
                    cq = cc * 4 + g4
                    mm8(pbz.t[:, cq * 64:(cq + 1) * 64], pbz, lambda k: wz.t[:, k, g4 * 128:(g4 + 1) * 128], lambda k: uTs[:, k, :], [wz, uT])
            t2 = TF()
            c.op("act", lambda e: e.activation(t2.t[:, :], pbz.t[:, :], AF.Tanh, scale=0.5), reads=[pbz], writes=[t2])
            c.op("dve", lambda e: e.scalar_tensor_tensor(zsT, t2.t[:, :], 1.0, pbz.t[:, :], ALU.add, ALU.mult), reads=[t2, pbz], writes=[kt])
            yield
            for half in range(2):
                Kf = f1k[half]
                Kf3 = Kf.t[:, :].rearrange("p (b f) -> p b f", b=8)
                c.dma("sp", Kf3, ck[half * 8:(half + 1) * 8, :, :].rearrange("b s f -> s b f"), dyo[half], writes=[Kf])
                for q4 in range(2):
                    kcd = B1K()
                    kcd5 = kcd.t[:, :].rearrange("p (b g u d) -> p b g u d", b=4, g=2, u=2)
                    for dup in range(2):
                        c.op("pool" if dup == 0 else "act",
                             (lambda e, dup=dup: e.tensor_copy(kcd5[:, :, :, dup, :], Kf3[:, q4 * 4:q4 * 4 + 4, :].rearrange("p b (g d) -> p b g d", g=2)))
                             if dup == 0 else
                             (lambda e, dup=dup: e.activation(kcd5[:, :, :, dup, :], Kf3[:, q4 * 4:q4 * 4 + 4, :].rearrange("p b (g d) -> p b g d", g=2), AF.Copy)),
                             reads=[Kf], writes=[kcd])
                    bkT = P(); bvT = bfv(bkT)
                    for bb in range(4):
                        for g in range(2):
                            i8 = bb * 2 + g
                            c.op("pe", lambda e, bb=bb, g=g, i8=i8: e.transpose(bvT[:, i8 * 128:(i8 + 1) * 128],
                                                                               kcd.t[:, (bb * 2 + g) * 128:(bb * 2 + g + 1) * 128], ident.t[:, :]),
                                 reads=[kcd, ident], writes=[bkT], signal=(i8 == 7))
                    b0 = half * 8 + q4 * 4
                    c.op("dve", lambda e: e.tensor_copy(KcT.t[:, b0 * 256:(b0 + 4) * 256], bvT[:, :]), reads=[bkT], writes=[KcT])
                    yield
            for half in range(2):
                Vf = f1k[half]
                Vf3 = Vf.t[:, :].rearrange("p (b f) -> p b f", b=8)
                c.dma("sp", Vf3, cv[half * 8:(half + 1) * 8, :, :].rearrange("b s f -> s b f"), dyo[half], writes=[Vf])
                c.op("act", lambda e: e.activation(vc[:, half * 8:(half + 1) * 8, :], Vf3, AF.Copy), reads=[Vf], writes=[gate])
                yield
            Enb = b1k[1]; PNb = b1k[0]
            c.op("pool", lambda e: e.memset(PNb.t[:, :], 0.0), writes=[PNb])
            for par in range(2):
                c.op("dve", lambda e, par=par: e.tensor_tensor(Enb.t[0:64, par * 512:(par + 1) * 512].rearrange("p (b h t) -> p b h t", b=16, h=8),
                                                              rawap(Rrep.t, par * 4 + 3, [[64, 64], [0, 16], [8, 8], [-1, 4]]),
                                                              rawap(bmask.t, 0, [[16, 64], [1, 16], [0, 8], [0, 4]]), ALU.mult),
                     reads=[Rrep, bmask], writes=[Enb])
            scC = [banks[6], banks[7]]
            scN = [banks[3], banks[4]]
            for par in range(2):
                rows = slice(par * 64, par * 64 + 64)
                n = 0
                for b_ in range(16):
                    for he in range(8):
                        hq = 2 * he + par; g = hq // 8
                        cs = slice(b_ * 32 + he * 4, b_ * 32 + he * 4 + 4)
                        n += 1
                        c.op("pe", lambda e, b_=b_, he=he, g=g, cs=cs: e.matmul(scC[par].t[:, cs], KcT4[rows, b_, g, :], bqTs[rows, he * 64 + b_ * 4:he * 64 + b_ * 4 + 4],
                                                                               start=True, stop=True), reads=[KcT, kt], writes=[scC[par]], signal=(n % 32 == 0))
                    if b_ % 4 == 3 and he == 7:
                        yield
                n = 0
                for b_ in range(16):
                    for he in range(8):
                        hq = 2 * he + par; g = hq // 8
                        cs = slice(b_ * 32 + he * 4, b_ * 32 + he * 4 + 4)
                        n += 1
                        c.op("pe", lambda e, b_=b_, he=he, g=g, cs=cs: e.matmul(scN[par].t[0:64, cs], kTn[rows, g * 64:(g + 1) * 64], bqTs[rows, he * 64 + b_ * 4:he * 64 + b_ * 4 + 4],
                                                                               start=True, stop=True), reads=[kt], writes=[scN[par]], signal=(n % 32 == 0))
                    if b_ % 4 == 3 and he == 7:
                        yield
            PC = [PTt[1], PTt[2]]
            for par in range(2):
                eC = TF(); eN = TF()
                c.op("act", lambda e: e.activation(eC.t[:, :], scC[par].t[:, :], AF.Exp, scale=0.125), reads=[scC[par]], writes=[eC])
                c.op("act", lambda e: e.activation(eN.t[0:64, :], scN[par].t[0:64, :], AF.Exp, scale=0.125), reads=[scN[par]], writes=[eN])
                c.op("dve", lambda e, par=par: e.tensor_tensor(PC[par].t[:, :].rearrange("p (b h t) -> p b h t", b=16, h=8),
                                                              eC.t[:, :].rearrange("p (b h t) -> p b h t", b=16, h=8),
                                                              rawap(Et.t, par * 256 + 128, [[4096, 128], [0, 16], [512, 8], [1, 4]]), ALU.mult),
                     reads=[eC, Et], writes=[PC[par]])
                c.op("pool", lambda e, par=par: e.tensor_tensor(PNb.t[0:64, par * 512:(par + 1) * 512], eN.t[0:64, :], Enb.t[0:64, par * 512:(par + 1) * 512], ALU.mult),
                     reads=[eN, Enb], writes=[PNb])
            pvT = banks[6]; pden = banks[7]
            for par in range(2):
                orow = slice(par * 64, par * 64 + 64)
                pc2 = PC[par].t[:, :]
                pn2 = PNb.t[:, par * 512:(par + 1) * 512]
                ov2 = pvT.t[orow, :]
                for b_ in range(16):
                    for g in range(2):
                        hs = slice(b_ * 32 + g * 16, b_ * 32 + g * 16 + 16)
                        c.op("pe", lambda e, b_=b_, g=g, hs=hs: e.matmul(ov2[:, hs], vc[:, b_, g * 64:(g + 1) * 64], pc2[:, hs], start=True, stop=False),
                             reads=[gate, PC[par]], writes=[pvT], signal=False)
                        c.op("pe", lambda e, b_=b_, g=g, hs=hs: e.matmul(ov2[:, hs], vnew[:, g * 64:(g + 1) * 64], pn2[:, hs], start=False, stop=True),
                             reads=[gate, PNb], writes=[pvT], signal=(b_ % 4 == 3 and g == 1))
                    if b_ % 4 == 3:
                        yield
                c.op("pe", lambda e: e.matmul(pden.t[orow, :], ones128.t[:, 0:64], PC[par].t[:, :], start=True, stop=False),
                     reads=[ones128, PC[par]], writes=[pden], signal=False)
                c.op("pe", lambda e: e.matmul(pden.t[orow, :], ones128.t[:, 0:64], PNb.t[:, par * 512:(par + 1) * 512], start=False, stop=True),
                     reads=[ones128, PNb], writes=[pden], signal=True)
            d_ = TF(); rd = TF(); o1 = TF()
            c.op("dve", lambda e: e.tensor_tensor(d_.t[:, :].rearrange("p (b h t) -> p b h t", b=16, h=8),
                                                  pden.t[:, :].rearrange("p (b h t) -> p b h t", b=16, h=8),
                                                  rawap(esinkP2.t, 0, [[8, 128], [0, 16], [1, 8], [0, 4]]), ALU.add), reads=[pden, esinkP2], writes=[d_])
            c.op("dve", lambda e: e.tensor_scalar(d_.t[:, :], d_.t[:, :], 2.0, None, ALU.mult), reads=[d_], writes=[d_])
            c.op("dve", lambda e: e.reciprocal(rd.t[:, :], d_.t[:, :]), reads=[d_], writes=[rd])
            c.op("dve", lambda e: e.tensor_tensor(o1.t[:, :], pvT.t[:, :], rd.t[:, :], ALU.mult), reads=[pvT, rd], writes=[o1])
            c.op("pool", lambda e: e.tensor_tensor(obT.t[:, 0:512].rearrange("p (h b t) -> p b h t", h=8, b=16),
                                                   o1.t[:, :].rearrange("p (b h t) -> p b h t", b=16, h=8),
                                                   zsT.rearrange("p (h b t) -> p b h t", h=8, b=16), ALU.mult), reads=[o1, kt], writes=[obT])

            yield

        oTb = banks[5]
        bstate["n"] = 3
        swa = swa_gen()
        for h in range(8):
            c.op("pe", lambda e, h=h: e.matmul(oTb.t[:, h * 64:(h + 1) * 64], v3v[:, 0, h * 128:(h + 1) * 128], aT.t[:, h, 0:64], start=(h == 0), stop=False, skip_group_check=True),
                 reads=[vv, aT], writes=[oTb], signal=False)
        for b_ in range(16):
            Sx = sx_list[b_ % 3]; dsx = sx_ds[b_ % 3]
            Sx3 = Sx.t[:, :].rearrange("p (h d) -> p h d", h=8)
            Sb = Sbf[b_ % 3]
            c.op("act", lambda e: e.activation(Sb.t[:, :, :], Sx3, AF.Copy), reads=[Sx], writes=[Sb])
            for h in range(8):
                last = (b_ == 15 and h == 7)
                c.op("pe", lambda e, h=h: e.matmul(oTb.t[:, h * 64 + b_ * 4:h * 64 + b_ * 4 + 4], Sb.t[:, h, :], qts[:, h * 64 + b_ * 4:h * 64 + b_ * 4 + 4],
                                                   start=False, stop=last, skip_group_check=True),
                     reads=[Sb, qt], writes=[oTb], signal=(h == 7))
            khm = oaT
            kho = (b_ % 2) * 1024
            c.op("act", lambda e: e.activation(khm.t[0:64, kho:kho + 1024], khtok, AF.Copy, scale=bmask.t[:, b_:b_ + 1]), reads=[qt, bmask], writes=[khm])
            for half in range(2):
                ps_ = P()
                for hh in range(4):
                    h = half * 4 + hh
                    c.op("pe", lambda e, h=h, hh=hh: e.matmul(ps_.t[:, hh * 128:(hh + 1) * 128], khm.t[0:64, kho + h * 128:kho + (h + 1) * 128], vs_[:, h * 128:(h + 1) * 128],
                                                             start=True, stop=True), reads=[khm, vv], writes=[ps_], signal=(hh == 3))
                P4b = bc(Pcs.rearrange("p (h b t) -> p h b t", h=8, b=16)[:, half * 4:half * 4 + 4, b_, 3], 128)
                Sh = Sx3[:, half * 4:half * 4 + 4, :]
                c.op("pool" if half == 0 else "dve", lambda e: e.tensor_tensor(Sh, Sh, P4b, ALU.mult), reads=[Sx] + S2, writes=[Sx])
                c.op("dve", lambda e: e.tensor_tensor(Sh, Sh, ps_.t[:, :].rearrange("p (h d) -> p h d", h=4), ALU.add), reads=[Sx, ps_], writes=[Sx])
            c.dma("act", ss_o[b_].rearrange("h p v -> p h v"), Sx3, dsx, reads=[Sx])
            if b_ + 2 < 16:
                issue_in(b_ + 2)
            if b_ == 13:
                load_w(wb_ab, 0, preload=True); load_w(wb_ab, 1024, preload=True)
                load_win(C_GA, preload=True); load_win(C_GB, preload=True)
            for _ in range(3):
                next(swa, None)
        for _ in swa:
            pass
        bstate["n"] = 5
        sqT = PTt[0]
        c.op("act", lambda e: e.activation(sqT.t[:, :], oTb.t[:, :], AF.Square), reads=[oTb], writes=[sqT])
        pssq = P()
        c.op("pe", lambda e: e.matmul(pssq.t[:, :], ones128.t[:, :], sqT.t[:, :], start=True, stop=True), reads=[ones128, sqT], writes=[pssq])
        rs = TF(); rstd = TF(); o1 = TF()
        c.op("dve", lambda e: e.tensor_scalar(rs.t[:, :], pssq.t[:, :], 0.25 / 128, EPS, ALU.mult, ALU.add), reads=[pssq], writes=[rs])
        yN = TF(); tN = TF()
        c.op("dve", lambda e: e.tensor_scalar(yN.t[:, :].bitcast(I32), rs.t[:, :].bitcast(I32), -0.5, 1597463007.0, ALU.mult, ALU.add),
             reads=[rs], writes=[yN])
        for it in range(3):
            c.op("dve", lambda e: e.tensor_tensor(tN.t[:, :], yN.t[:, :], yN.t[:, :], ALU.mult), reads=[yN], writes=[tN])
            c.op("dve", lambda e: e.tensor_tensor(tN.t[:, :], tN.t[:, :], rs.t[:, :], ALU.mult), reads=[tN, rs], writes=[tN])
            c.op("dve", lambda e: e.tensor_scalar(tN.t[:, :], tN.t[:, :], -0.5, 1.5, ALU.mult, ALU.add), reads=[tN], writes=[tN])
            dstN = yN if it < 2 else rstd
            c.op("dve", lambda e: e.tensor_tensor(dstN.t[:, :], yN.t[:, :], tN.t[:, :], ALU.mult), reads=[yN, tN], writes=[dstN])
        c.op("dve", lambda e: e.tensor_tensor(o1.t[:, :], oTb.t[:, :], rstd.t[:, :], ALU.mult), reads=[oTb, rstd], writes=[o1])
        c.op("pool", lambda e: e.tensor_tensor(oaT.t[:, 0:512], o1.t[:, :], gT2, ALU.mult), reads=[o1] + S2, writes=[oaT])

        mark('s.p4')
        phase4(NS, 64, ysm, 0)
        mark('end')

    late_setup_a()
    cast_stage(0)
    for ti in range(NTL if STOP > 0 else 0):
        c.redirect = (ti == 0)
        prompt_tile(ti)
        c.redirect = False
    if STOP > 4 and NTL == NTILES:
        uT3e = v3(uT, 8)
        pv = P(); pk = P()
        kvout_t = F1K()
        kvout3 = kvout_t.t[:, 0:256].rearrange("p (a f) -> p a f", a=2)
        mm8(pv.t[:, 0:128], pv, lambda k: uT3e[:, k, 384:512], lambda k: wbv.t[:, k, :], [wbv, uT])
        for g in range(2):
            mm8(pk.t[:, g * 64:(g + 1) * 64], pk, lambda k: uT3e[:, k, 384:512], lambda k: wk2.t[:, k, g, 0, :], [wk2, uT])
        c.op("dve", lambda e: e.tensor_copy(kvout3[:, 1, :], pv.t[:, 0:128]), reads=[pv], writes=[kvout_t])
        c.op("act", lambda e: e.activation(kvout3[:, 0, :], pk.t[:, 0:128], AF.Copy), reads=[pk], writes=[kvout_t])
        d_kv = dyo[f1k.index(kvout_t)]
        c.dma("sp", kp_o[:, :], kvout3[:, 0, :], d_kv, reads=[kvout_t])
        c.dma("sp", vp_o[:, :], kvout3[:, 1, :], d_kv, reads=[kvout_t])
    c.dma("sp", sp_o[:, :, :].rearrange("h p v -> p h v"), S.t[:, :, :], c.dsem("spo"), reads=S2)
    sample_path()
    c.finish()
    return nc, c


def _rel_bucket(rel):
    n = np.maximum(rel, 0)
    nf = np.maximum(n, 1).astype(np.float32)
    large = 16 + (np.log(nf / np.float32(16)) / np.float32(math.log(128 / 16)) * np.float32(16)).astype(np.int32)
    large = np.minimum(large, 31)
    return np.where(n < 16, n, large)


def _consts():
    s = np.arange(128)[:, None]; t = np.arange(128)[None, :]
    mask64 = ((s // 64 == t // 64) & (s <= t)).astype(np.float32)
    scanm = np.zeros((128, 512), np.float32); scanm[:, ::64] = 1.0
    grev = np.zeros((32, 384), np.float32); valid = np.zeros((16, 384), np.float32)
    for i in range(384):
        rel = (383 - i) - 128
        if 0 <= rel < 128:
            grev[int(_rel_bucket(np.array(rel))), i] = 1.0
            valid[:, i] = 1.0
    p = np.arange(64)[:, None]
    bmask = (p // 4 == np.arange(16)[None, :]).astype(np.float32)
    s4 = np.arange(64)[:, None]; t4 = np.arange(64)[None, :]
    mask4 = ((s4 // 4 == t4 // 4) & (s4 <= t4)).astype(np.float32)
    return dict(k_ident=np.eye(128, dtype=np.float32), k_mask64=mask64, k_scanm=scanm, k_grev=grev, k_valid=valid,
                k_bmask=bmask, k_mask4=mask4)


_CACHE = {}


def kernel(x_prompt, x_sample, state_hgrn, cache_swa_k, cache_swa_v, p_prompt, p_sample,
           norm_pre, w_in, hgrn_lb, hgrn_norm, attn_sink, rel_bias, w_pa, w_pb, w_o,
           norm_post, w_ple, w_ple_gate):
    f = lambda a: np.ascontiguousarray(np.asarray(a), dtype=np.float32)
    x_prompt, x_sample, state_hgrn, cache_swa_k, cache_swa_v, p_prompt, p_sample = map(
        f, (x_prompt, x_sample, state_hgrn, cache_swa_k, cache_swa_v, p_prompt, p_sample))
    if "nc" not in _CACHE:
        _CACHE["nc"] = build_program()[0]
    nc = _CACHE["nc"]
    consts = _consts()
    shared = dict(w_in=f(w_in)[0],
                  w_ab=np.ascontiguousarray(np.concatenate([f(w_pa)[0], f(w_pb)[0]], axis=1)),
                  w_og=np.ascontiguousarray(np.concatenate([f(w_o)[0], f(w_ple_gate)[0]], axis=1)),
                  w_ple=f(w_ple)[0], norm_pre=f(norm_pre), hgrn_lb=f(hgrn_lb), hgrn_norm=f(hgrn_norm),
                  attn_sink=f(attn_sink), rel_bias=f(rel_bias), norm_post=f(norm_post), **consts)
    in_maps = []
    for i in range(8):
        b = slice(16 * i, 16 * i + 16)
        m = dict(shared)
        m.update(xp=x_prompt[i], pp=p_prompt[0, i], xsm=x_sample[b].reshape(64, 1024), psm=p_sample[0, b].reshape(64, 256),
                 st0=state_hgrn[0, b], ck=cache_swa_k[0, b].reshape(16, 128, 128), cv=cache_swa_v[0, b].reshape(16, 128, 128))
        in_maps.append(m)
    res = run_bass_kernel_spmd(nc, in_maps, core_ids=list(range(8)))
    R = res.results
    y_prompt = np.stack([R[i]["yp"] for i in range(8)])
    y_sample = np.concatenate([R[i]["ysm"].reshape(16, 4, 1024) for i in range(8)])
    s_prompt = np.stack([R[i]["sp_o"] for i in range(8)])[None]
    s_sample = np.concatenate([R[i]["ss_o"] for i in range(8)])[None]
    k_prompt = np.stack([R[i]["kp_o"].reshape(128, 2, 64) for i in range(8)])[None]
    v_prompt = np.stack([R[i]["vp_o"].reshape(128, 2, 64) for i in range(8)])[None]
    k_sample = np.concatenate([R[i]["ks_o"].reshape(16, 128, 2, 64) for i in range(8)])[None]
    v_sample = np.concatenate([R[i]["vs_o"].reshape(16, 128, 2, 64) for i in range(8)])[None]
    return tuple(np.ascontiguousarray(a, dtype=np.float32) for a in
                 (y_prompt, y_sample, s_prompt, s_sample, k_prompt, v_prompt, k_sample, v_sample))
```

```python
import math
import numpy as np
import concourse.bass as bass
import concourse.mybir as mybir
from concourse.bass_utils import run_bass_kernel_spmd

F32 = mybir.dt.float32
BF16 = mybir.dt.bfloat16
I32 = mybir.dt.int32
AF = mybir.ActivationFunctionType
ALU = mybir.AluOpType
AX = mybir.AxisListType
EPS = 1e-6
NT = 512
NTILES = 4
USE_POOL_DIV = False
FOLD_WAIT = True
STRICT_SAME_ENGINE = True
MARKS = []
IN_TOTAL = 9472
C_AQ, C_AF, C_AI, C_AOG, C_AZ, C_BQ, C_BK, C_BV, C_BZ, C_GA, C_GB = (
    0, 1024, 2048, 3072, 4096, 5120, 6144, 6272, 6400, 7424, 8448)


class DSem:
    def __init__(self, h):
        self.h = h
        self.total = 0


class T:
    def __init__(self, t, name=""):
        self.t = t
        self.name = name
        self.w = None
        self.r = {}


class Ctx:
    def __init__(self, nc):
        self.nc = nc
        self.eng = {"pe": nc.tensor, "act": nc.scalar, "dve": nc.vector,
                    "pool": nc.gpsimd, "sp": nc.sync}
        self.sem = {}
        self.cnt = {}
        for e in ("pe", "act", "dve", "pool"):
            self.sem[e] = nc.alloc_semaphore("c_" + e)
            self.cnt[e] = 0
        self.seen = {e: {} for e in self.eng}
        self.dsems = []
        self.nwait = 0
        self.nins = {e: 0 for e in self.eng}
        self.redirect = False

    def sb(self, name, shape, dtype):
        return T(self.nc.alloc_sbuf_tensor(name, list(shape), dtype), name)

    def dsem(self, name):
        d = DSem(self.nc.alloc_semaphore("d_" + name))
        self.dsems.append(d)
        return d

    def _need(self, needs, key, val):
        if isinstance(key, DSem):
            needs[key] = key.total
        elif needs.get(key, 0) < val:
            needs[key] = val

    def _sync(self, e, reads, writes, fold_ok=False):
        needs = {}
        for t in reads:
            if t.w is not None:
                self._need(needs, t.w[0], t.w[1])
        same_ok = (e == "pe") or not STRICT_SAME_ENGINE
        for t in writes:
            if t.w is not None and (t.w[0] != e or not same_ok):
                self._need(needs, t.w[0], t.w[1])
            for k, v in t.r.items():
                if k != e or not same_ok:
                    self._need(needs, k, v)
        eng = self.eng[e]
        pend = [(k, v) for k, v in needs.items() if self.seen[e].get(k, 0) < v]
        fold = None
        if fold_ok and FOLD_WAIT and pend:
            fold = pend.pop()
        for k, v in pend:
            eng.wait_ge(k.h if isinstance(k, DSem) else self.sem[k], v)
            self.seen[e][k] = v
            self.nwait += 1
        if fold is not None:
            self.seen[e][fold[0]] = fold[1]
            return (fold[0].h if isinstance(fold[0], DSem) else self.sem[fold[0]], fold[1])
        return None

    def op(self, e, fn, reads=(), writes=(), signal=True, nofold=False):
        if e == "pool" and self.redirect:
            e = "dve"
        fw = self._sync(e, reads, writes, fold_ok=(e in ("act", "dve", "pool") and not nofold))
        ins = fn(self.eng[e])
        if fw is not None:
            ins._wait_ge(fw[0], fw[1])
        self.nins[e] += 1
        if signal:
            self.cnt[e] += 1
            ins.then_inc(self.sem[e], 1)
            val = self.cnt[e]
        else:
            val = self.cnt[e] + 1
        for t in reads:
            if t.r.get(e, 0) < val:
                t.r[e] = val
        for t in writes:
            t.w = (e, val)
            t.r = {}
        return ins

    def dma(self, q, out_ap, in_ap, ds, reads=(), writes=(), **kw):
        self._sync(q, reads, writes)
        ins = self.eng[q].dma_start(out=out_ap, in_=in_ap, **kw)
        self.nins[q] += 1
        ins.then_inc(ds.h, 16)
        ds.total += 16
        for t in reads:
            t.r[ds] = None
        for t in writes:
            t.w = (ds, None)
            t.r = {}
        return ins

    def finish(self):
        sp = self.eng["sp"]
        for d in self.dsems:
            if d.total > 0:
                sp.wait_ge(d.h, d.total)
        for e in ("pe", "act", "dve", "pool"):
            if self.cnt[e] > 0:
                sp.wait_ge(self.sem[e], self.cnt[e])


def bc(a, n):
    return bass.AP(a.tensor, a.offset, [list(d) for d in a.ap] + [[0, n]])


def bcmid(a, n):
    d = [list(x) for x in a.ap]
    return bass.AP(a.tensor, a.offset, d[:-1] + [[0, n]] + d[-1:])


def rawap(t, off, dims):
    return bass.AP(t, off, [list(d) for d in dims])


def build_program():
    nc = bass.Bass("TRN2", target_bir_lowering=False)
    c = Ctx(nc)
    STOP = 99
    NTL = NTILES

    def DI(name, shape, dt=F32):
        return nc.dram_tensor(name, list(shape), dt, kind="ExternalInput")

    def DO(name, shape, dt=F32):
        return nc.dram_tensor(name, list(shape), dt, kind="ExternalOutput")

    def DS(name, shape, dt):
        return T(nc.dram_tensor(name, list(shape), dt, kind="Internal"), name)

    xp = DI("xp", [2048, 1024]); pp = DI("pp", [2048, 256])
    xsm = DI("xsm", [64, 1024]); psm = DI("psm", [64, 256])
    st0 = DI("st0", [16, 8, 128, 128]); ck = DI("ck", [16, 128, 128]); cv = DI("cv", [16, 128, 128])
    w_in = DI("w_in", [1024, IN_TOTAL]); w_pa = DI("w_pa", [1024, 1024]); w_pb = DI("w_pb", [1024, 1024])
    w_o = DI("w_o", [1024, 1024]); w_pg = DI("w_pg", [1024, 1024]); w_ple = DI("w_ple", [256, 1024])
    norm_pre = DI("norm_pre", [1, 1024]); hgrn_lb = DI("hgrn_lb", [2, 1024]); hgrn_norm = DI("hgrn_norm", [1, 1024])
    attn_sink = DI("attn_sink", [1, 16]); rel_bias = DI("rel_bias", [32, 16]); norm_post = DI("norm_post", [1, 1024])
    k_ident = DI("k_ident", [128, 128]); k_mask64 = DI("k_mask64", [128, 128]); k_scanm = DI("k_scanm", [128, 512])
    k_grev = DI("k_grev", [32, 384]); k_valid = DI("k_valid", [16, 384])
    k_bmask = DI("k_bmask", [64, 16]); k_mask4 = DI("k_mask4", [64, 64])
    yp = DO("yp", [2048, 1024]); ysm = DO("ysm", [64, 1024])
    sp_o = DO("sp_o", [8, 128, 128]); ss_o = DO("ss_o", [16, 8, 128, 128])
    kp_o = DO("kp_o", [128, 128]); vp_o = DO("vp_o", [128, 128])
    ks_o = DO("ks_o", [16, 128, 128]); vs_o = DO("vs_o", [16, 128, 128])
    wb_all = nc.dram_tensor("wb_all", [1024, IN_TOTAL], BF16, kind="Internal")
    wb_in = [T(wb_all, "wb_in%d" % i) for i in range(10)]
    wb_pa = DS("wb_pa", [1024, 1024], BF16); wb_pb = DS("wb_pb", [1024, 1024], BF16)
    wb_o = DS("wb_o", [1024, 1024], BF16); wb_pg = DS("wb_pg", [1024, 1024], BF16)
    wb_ple = DS("wb_ple", [256, 1024], BF16)
    escr = DS("escr", [16, 384], F32)

    CG = 2368
    wb_grp = [T(wb_all, "wb_grp%d" % q) for q in range(4)]
    cw_ds = [c.dsem("cw%d" % q) for q in range(4)]
    cw_other = {d.name: c.dsem("cw_" + d.name) for d in (wb_pa, wb_pb, wb_o, wb_pg, wb_ple)}
    cast_done = set()

    def cast_grp(q):
        if ("g", q) in cast_done:
            return
        cast_done.add(("g", q))
        for r in range(8):
            c.dma("pool", wb_all[r * 128:(r + 1) * 128, q * CG:(q + 1) * CG], w_in[r * 128:(r + 1) * 128, q * CG:(q + 1) * CG], cw_ds[q], writes=[wb_grp[q]])

    def cast_w(src, dst):
        if dst.name in cast_done:
            return
        cast_done.add(dst.name)
        for r in range(src.shape[0] // 128):
            c.dma("pool", dst.t[r * 128:(r + 1) * 128, :], src[r * 128:(r + 1) * 128, :], cw_other[dst.name], writes=[dst])

    def cast_stage(k):
        if k == 0:
            for q in range(4):
                cast_grp(q)
            cast_w(w_pa, wb_pa); cast_w(w_pb, wb_pb); cast_w(w_o, wb_o); cast_w(w_pg, wb_pg); cast_w(w_ple, wb_ple)

    banks = [T(nc.alloc_psum_tensor("bank%d" % i, [128, 512], F32), "bank%d" % i) for i in range(8)]
    bstate = {"i": 0, "n": 5}

    def P():
        lst = bstate.get("list")
        if lst is not None:
            b = banks[lst[bstate["i"] % len(lst)]]
        else:
            b = banks[bstate["i"] % bstate["n"]]
        bstate["i"] += 1
        return b

    def bfv(b):
        return b.t[:, :].bitcast(BF16)

    d_const = c.dsem("const")
    ident = c.sb("ident", [128, 128], BF16)
    mask64 = c.sb("mask64", [128, 128], BF16)
    scanm = c.sb("scanm", [128, 512], F32)
    gpre = c.sb("gpre", [128, 1024], F32); gpost = c.sb("gpost", [128, 1024], F32); gn125 = c.sb("gn125", [128, 1024], F32)
    esink = c.sb("esink", [128, 16], F32)
    neghalf = c.sb("neghalf", [128, 16], F32)
    lbt = c.sb("lbt", [128, 2, 8], F32); th = c.sb("th", [128, 8], F32)
    c0 = c.sb("c0", [128, 8], F32); c1 = c.sb("c1", [128, 8], F32); nc1 = c.sb("nc1", [128, 8], F32)
    relb = c.sb("relb", [32, 16], F32); grev = c.sb("grev", [32, 384], F32); valid = c.sb("valid", [16, 384], F32)
    ev = c.sb("ev", [16, 384], F32)
    Et = c.sb("Et", [128, 16, 256], BF16)
    tmpf = [c.sb("tmpf%d" % i, [128, 512], F32) for i in range(8)]
    tstate = {"i": 0}

    def TF():
        t = tmpf[tstate["i"] % len(tmpf)]
        tstate["i"] += 1
        return t

    f1k = [c.sb("f1k%d" % i, [128, 1024], F32) for i in range(2)]
    b1k = [c.sb("b1k%d" % i, [128, 1024], BF16) for i in range(3)]
    fstate = {"f": 0, "b": 0}

    def F1K():
        t = f1k[fstate["f"] % 2]; fstate["f"] += 1; return t

    def B1K():
        t = b1k[fstate["b"] % 3]; fstate["b"] += 1; return t

    yo = f1k
    dyo = [c.dsem("yo%d" % i) for i in range(2)]
    ones128 = c.sb("ones128", [128, 128], BF16)
    bmask = c.sb("bmask", [64, 16], F32)
    mask4 = c.sb("mask4", [64, 64], BF16)
    gnT = c.sb("gnT", [128, 8], F32)
    esinkP2 = c.sb("esinkP2", [128, 8], F32)
    Rrep = c.sb("Rrep", [64, 16, 4], F32)

    c.dma("sp", scanm.t[:, :], k_scanm[:, :], d_const, writes=[scanm])
    c.dma("sp", gpre.t[:, :], rawap(norm_pre, 0, [[0, 128], [1, 1024]]), d_const, writes=[gpre])
    c.dma("sp", gpost.t[:, :], rawap(norm_post, 0, [[0, 128], [1, 1024]]), d_const, writes=[gpost])
    c.dma("sp", gn125.t[:, :], rawap(hgrn_norm, 0, [[0, 128], [1, 1024]]), d_const, writes=[gn125])
    c.dma("sp", esink.t[:, :], rawap(attn_sink, 0, [[0, 128], [1, 16]]), d_const, writes=[esink])
    c.dma("sp", lbt.t[:, :, :], rawap(hgrn_lb, 0, [[1, 128], [1024, 2], [128, 8]]), d_const, writes=[lbt],
          allow_slow_non_contiguous=True)
    c.dma("sp", relb.t[:, :], rel_bias[:, :], d_const, writes=[relb])
    c.dma("sp", grev.t[:, :], k_grev[:, :], d_const, writes=[grev])
    c.dma("sp", valid.t[:, :], k_valid[:, :], d_const, writes=[valid])
    c.dma("sp", tmpf[0].t[:, 0:128], k_ident[:, :], d_const, writes=[tmpf[0]])
    c.dma("sp", tmpf[1].t[:, 0:128], k_mask64[:, :], d_const, writes=[tmpf[1]])
    c.dma("sp", bmask.t[:, :], k_bmask[:, :], d_const, writes=[bmask])
    c.dma("sp", tmpf[2].t[0:64, 0:64], k_mask4[:, :], d_const, writes=[tmpf[2]])
    c.dma("sp", gnT.t[:, :], rawap(hgrn_norm, 0, [[1, 128], [128, 8]]), d_const, writes=[gnT], allow_slow_non_contiguous=True)

    c.op("dve", lambda e: e.tensor_scalar(gn125.t[:, :], gn125.t[:, :], 0.125, None, ALU.mult), reads=[gn125], writes=[gn125])
    c.op("act", lambda e: e.activation(esink.t[:, :], esink.t[:, :], AF.Exp), reads=[esink], writes=[esink])
    c.op("pool", lambda e: e.memset(neghalf.t[:, :], -0.5), writes=[neghalf])
    negone = c.sb("negone", [128, 2], F32)
    c.op("pool", lambda e: e.memset(negone.t[:, :], -1.0), writes=[negone])
    c.op("pool", lambda e: e.memset(ones128.t[:, :], 1.0), writes=[ones128])
    c.op("dve", lambda e: e.tensor_tensor(th.t[:, :], lbt.t[:, 0, :], lbt.t[:, 1, :], ALU.subtract), reads=[lbt], writes=[th])
    c.op("act", lambda e: e.activation(th.t[:, :], th.t[:, :], AF.Tanh, scale=0.5), reads=[th], writes=[th])
    c.op("dve", lambda e: e.tensor_scalar(c1.t[:, :], th.t[:, :], -0.25, 0.25, ALU.mult, ALU.add), reads=[th], writes=[c1])
    c.op("dve", lambda e: e.tensor_scalar(c0.t[:, :], th.t[:, :], 0.25, 0.75, ALU.mult, ALU.add), reads=[th], writes=[c0])
    c.op("dve", lambda e: e.tensor_scalar(nc1.t[:, :], th.t[:, :], 0.25, -0.25, ALU.mult, ALU.add), reads=[th], writes=[nc1])
    c.op("dve", lambda e: e.tensor_copy(ident.t[:, :], tmpf[0].t[:, 0:128]), reads=[tmpf[0]], writes=[ident])
    c.op("dve", lambda e: e.tensor_copy(mask64.t[:, :], tmpf[1].t[:, 0:128]), reads=[tmpf[1]], writes=[mask64])
    c.op("dve", lambda e: e.tensor_copy(mask4.t[:, :], tmpf[2].t[0:64, 0:64]), reads=[tmpf[2]], writes=[mask4])
    c.op("dve", lambda e: e.tensor_scalar(gnT.t[:, :], gnT.t[:, :], 0.125, None, ALU.mult), reads=[gnT], writes=[gnT])
    c.op("dve", lambda e: e.tensor_copy(esinkP2.t[0:64, :], esink.t[0:64, 0:16:2]), reads=[esink], writes=[esinkP2])
    c.op("dve", lambda e: e.tensor_copy(esinkP2.t[64:128, :], esink.t[64:128, 1:16:2]), reads=[esink], writes=[esinkP2])

    nw = c.sb("nw", [128, 16], F32)

    def rsqrt(dst, src, R, n, tiles):
        if not c.redirect:
            c.op("pool", lambda e: e.tensor_tensor(dst, src, neghalf.t[R, 0:n], ALU.pow), reads=tiles + [neghalf], writes=tiles)
            return
        y = nw.t[R, 0:n]; t = nw.t[R, 8:8 + n]
        c.op("dve", lambda e: e.tensor_scalar(y.bitcast(I32), src.bitcast(I32), -0.5, 1597463007.0, ALU.mult, ALU.add),
             reads=tiles, writes=[nw])
        for it in range(3):
            c.op("dve", lambda e: e.tensor_tensor(t, y, y, ALU.mult), reads=[nw], writes=[nw])
            c.op("dve", lambda e: e.tensor_tensor(t, t, src, ALU.mult), reads=[nw] + tiles, writes=[nw])
            c.op("dve", lambda e: e.tensor_scalar(t, t, -0.5, 1.5, ALU.mult, ALU.add), reads=[nw], writes=[nw])
            if it < 2:
                c.op("dve", lambda e: e.tensor_tensor(y, y, t, ALU.mult), reads=[nw], writes=[nw])
            else:
                c.op("dve", lambda e: e.tensor_tensor(dst, y, t, ALU.mult), reads=[nw], writes=tiles)

    def late_setup_a():
        pb_ = P()
        c.op("pe", lambda e: e.matmul(pb_.t[0:16, 0:384], relb.t[:, :], grev.t[:, :], start=True, stop=True),
             reads=[relb, grev], writes=[pb_])
        c.op("act", lambda e: e.activation(ev.t[:, :], pb_.t[0:16, 0:384], AF.Exp), reads=[pb_], writes=[ev])
        c.op("dve", lambda e: e.tensor_tensor(ev.t[:, :], ev.t[:, :], valid.t[:, :], ALU.mult), reads=[ev, valid], writes=[ev])
        d_e = c.dsem("escr")
        c.dma("sp", escr.t[:, :], ev.t[:, :], d_e, reads=[ev], writes=[escr])

    def late_setup():
        if late['done']:
            return
        late['done'] = True
        d_rr = c.dsem("rrep")
        for b_ in range(16):
            c.dma("sp", Rrep.t[b_ * 4:(b_ + 1) * 4, :, :], rawap(escr.t, 252, [[1, 4], [384, 16], [1, 4]]), d_rr, reads=[escr], writes=[Rrep])
        for gq in range(4):
            Rt = f1k[gq % 2]
            c.dma("sp", Rt.t[:, :].rearrange("p (h t) -> p h t", h=4), rawap(escr.t, gq * 4 * 384, [[1, 128], [384, 4], [1, 256]]),
                  dyo[gq % 2], reads=[escr], writes=[Rt])
            for hh in range(4):
                src = rawap(Rt.t, hh * 256 + 255, [[1024, 128], [-1, 256]])
                c.op("dve", lambda e, hh=hh, src=src: e.tensor_copy(Et.t[:, gq * 4 + hh, :], src), reads=[Rt], writes=[Et])


    late = {'done': False}

    wk2 = c.sb("wk2", [128, 8, 2, 2, 64], BF16)
    wbv = c.sb("wbv", [128, 8, 128], BF16)
    wple = c.sb("wple", [128, 2, 1024], BF16)
    d_wres = c.dsem("wres")
    d_wple = c.dsem("wple")
    kvsrc = wb_grp[C_BK // CG]
    assert (C_BV + 127) // CG == C_BK // CG
    resident = {"kv": False, "ple": False}

    def need_kv():
        if resident["kv"]:
            return
        resident["kv"] = True
        for dup in range(2):
            for g in range(2):
                c.dma("sp", wk2.t[:, :, g, dup, :],
                      wb_all[:, C_BK + g * 64:C_BK + g * 64 + 64].rearrange("(k p) d -> p k d", p=128),
                      d_wres, reads=[kvsrc], writes=[wk2])
        c.dma("sp", wbv.t[:, :, :], wb_all[:, C_BV:C_BV + 128].rearrange("(k p) n -> p k n", p=128), d_wres, reads=[kvsrc], writes=[wbv])

    def need_ple():
        if resident["ple"]:
            return
        resident["ple"] = True
        c.dma("sp", wple.t[:, :, :], wb_ple.t[:, :].rearrange("(k p) n -> p k n", p=128), d_wple, reads=[wb_ple], writes=[wple])

    NW = 5
    wbufs = [c.sb("wbuf%d" % i, [128, 8, 512], BF16) for i in range(NW)]
    wds = [c.dsem("wbuf%d" % i) for i in range(NW)]
    wstate = {"i": 0}

    def load_w(src_t, col0, extra=()):
        i = wstate["i"] % NW
        wstate["i"] += 1
        c.dma("sp", wbufs[i].t[:, :, :], src_t.t[:, col0:col0 + 512].rearrange("(k p) n -> p k n", p=128),
              wds[i], reads=[src_t] + list(extra), writes=[wbufs[i]])
        return wbufs[i]

    def load_win(col):
        a, b = col // CG, (col + 511) // CG
        return load_w(wb_grp[a], col, extra=([wb_grp[b]] if b != a else []))

    big = [c.sb("big%d" % i, [128, 4096], BF16) for i in range(7)]

    def v3(t, a):
        return t.t[:, :].rearrange("p (a b) -> p a b", a=a)

    uT, oaT = big[0], big[1]
    qt = bqT = mT = big[2]
    kt = zs2 = x1T = big[3]
    ktok = obT = big[4]
    vv = big[5]
    gate = big[6]

    xs = [c.sb("xs%d" % i, [128, 1024], F32) for i in range(4)]
    dxs = [c.dsem("xs%d" % i) for i in range(4)]
    preloaded = set()
    for blk_ in range(4):
        c.dma("sp", xs[blk_].t[:, :], xp[blk_ * 128:(blk_ + 1) * 128, :], dxs[blk_], writes=[xs[blk_]])
        preloaded.add(blk_)
    pf = [c.sb("pf%d" % i, [128, 256], F32) for i in range(2)]
    dpf = [c.dsem("pf%d" % i) for i in range(2)]
    pbt = [c.sb("pbt%d" % i, [128, 256], BF16) for i in range(2)]
    pT = c.sb("pT", [128, 2, 512], BF16)
    st4 = [c.sb("st4_%d" % i, [128, 16], F32) for i in range(4)]
    Pl = c.sb("Pl", [128, 8, 8], F32)
    S = c.sb("S", [128, 8, 128], F32)
    S2 = [T(S.t, "S_lo"), T(S.t, "S_hi")]
    Sbf = [c.sb("Sbf%d" % i, [128, 8, 128], BF16) for i in range(3)]
    sstate = {"i": 0}
    attT = [c.sb("attT%d" % i, [128, 8, 128], BF16) for i in range(2)]
    kT2 = c.sb("kT2", [128, 2, 640], BF16)
    vaug = c.sb("vaug", [128, 5, 2, 65], BF16)
    PTt = [c.sb("PTt%d" % i, [128, 512], BF16) for i in range(4)]
    ptstate = {"i": 0}
    den = c.sb("den", [128, 16], F32); rden = c.sb("rden", [128, 16], F32)
    d_misc = c.dsem("misc")

    c.op("dve", lambda e: e.memset(S.t[:, :, :], 0.0), writes=S2)
    c.op("pool", lambda e: e.memset(Sbf[0].t[:, :, :], 0.0), writes=[Sbf[0]])
    c.op("pool", lambda e: e.memset(vaug.t[:, :, :, :], 1.0), writes=[vaug])

    def transposes(src, npart, dst_fn, evac="act"):
        b = P()
        bv = bfv(b)
        for k in range(8):
            c.op("pe", lambda e, k=k: e.transpose(bv[:, k * npart:(k + 1) * npart] if False else bv[:, k * 128:k * 128 + npart],
                                                  src.t[0:npart, k * 128:(k + 1) * 128], ident.t[0:npart, 0:npart]),
                 reads=[src, ident], writes=[b], signal=(k == 7))
        return b, bv

    def mm8(bank_ap, bank, lhs_fn, rhs_fn, reads, nk=8):
        for k in range(nk):
            c.op("pe", lambda e, k=k: e.matmul(bank_ap, lhs_fn(k), rhs_fn(k), start=(k == 0), stop=(k == nk - 1)),
                 reads=reads, writes=[bank], signal=(k == nk - 1))

    def fm(t, ntok):
        return t.t[:, 0:8 * ntok].rearrange("p (k t) -> p k t", k=8)

    def phase0(ntok, bp, x_dram, p_dram, row0):
        for _ in phase0_gen(ntok, bp, x_dram, p_dram, row0):
            pass

    def phase0_gen(ntok, bp, x_dram, p_dram, row0):
        uTv = fm(uT, ntok)
        R = slice(0, bp)
        for blk in range(ntok // bp):
            r0 = row0 + blk * bp
            tcols = slice(blk * bp, (blk + 1) * bp)
            x_ = xs[blk]
            if x_dram is xp and row0 == 0 and blk in preloaded:
                preloaded.discard(blk)
            else:
                c.dma("sp", x_.t[R, :], x_dram[r0:r0 + bp, :], dxs[blk], writes=[x_])
            st = st4[blk]
            junk = B1K()
            c.op("act", lambda e: e.activation(junk.t[R, :], x_.t[R, :], AF.Square, accum_out=st.t[R, 0:1]),
                 reads=[x_], writes=[junk, st], nofold=True)
            c.op("dve", lambda e: e.tensor_scalar(st.t[R, 1:2], st.t[R, 0:1], 1.0 / 1024, EPS, ALU.mult, ALU.add),
                 reads=[st], writes=[st])
            rsqrt(st.t[R, 2:3], st.t[R, 1:2], R, 1, [st])
            ub = B1K()
            c.op("dve", lambda e: e.scalar_tensor_tensor(ub.t[R, :], x_.t[R, :], st.t[R, 2:3], gpre.t[R, :], ALU.mult, ALU.mult),
                 reads=[x_, st, gpre], writes=[ub])
            b, bv = transposes(ub, bp, None)
            c.op("act", lambda e: e.activation(uTv[:, :, tcols],
                                               bv[:, :].rearrange("p (k t) -> p k t", k=8)[:, :, 0:bp], AF.Copy),
                 reads=[b], writes=[uT])
            pf_ = pf[blk % 2]; pb_ = pbt[blk % 2]
            c.dma("sp", pf_.t[R, :], p_dram[r0:r0 + bp, :], dpf[blk % 2], writes=[pf_])
            c.op("pool", lambda e: e.tensor_copy(pb_.t[R, :], pf_.t[R, :]), reads=[pf_], writes=[pb_])
            b2 = P(); bv2 = bfv(b2)
            for k in range(2):
                c.op("pe", lambda e, k=k: e.transpose(bv2[:, k * 128:k * 128 + bp], pb_.t[R, k * 128:(k + 1) * 128], ident.t[R, R]),
                     reads=[pb_, ident], writes=[b2], signal=(k == 1))
            c.op("dve", lambda e: e.tensor_copy(pT.t[:, :, tcols],
                                                bv2[:, 0:256].rearrange("p (k t) -> p k t", k=2)[:, :, 0:bp]),
                 reads=[b2], writes=[pT])
            yield

    def tokmajor_v_gate(ntok, bp, side=None):
        uTv = fm(uT, ntok)
        R = slice(0, bp)
        nblk = ntok // bp
        v3v = v3(vv, 4); g3 = v3(gate, 4)
        for cc in range(2):
            wv = load_win(C_AI + cc * 512)
            for blk in range(nblk):
                tcols = slice(blk * bp, (blk + 1) * bp)
                pv = P()
                mm8(pv.t[R, :], pv, lambda k: uTv[:, k, tcols], lambda k: wv.t[:, k, :], [wv, uT])
                c.op("act", lambda e: e.activation(v3v[R, blk, cc * 512:(cc + 1) * 512], pv.t[R, :], AF.Copy), reads=[pv], writes=[vv])
        if bp == 64:
            return
        for cc in range(2):
            wg = load_win(C_AOG + cc * 512)
            wz = load_win(C_AZ + cc * 512)
            for blk in range(nblk):
                tcols = slice(blk * bp, (blk + 1) * bp)
                pg = P(); pz = P()
                mm8(pg.t[R, :], pg, lambda k: uTv[:, k, tcols], lambda k: wg.t[:, k, :], [wg, uT])
                mm8(pz.t[R, :], pz, lambda k: uTv[:, k, tcols], lambda k: wz.t[:, k, :], [wz, uT])
                t1 = TF(); t2 = TF(); B1 = TF()
                c.op("act", lambda e: e.activation(t1.t[R, :], pg.t[R, :], AF.Tanh, scale=0.5), reads=[pg], writes=[t1])
                c.op("act", lambda e: e.activation(t2.t[R, :], pz.t[R, :], AF.Tanh, scale=0.5), reads=[pz], writes=[t2])
                c.op("dve", lambda e: e.scalar_tensor_tensor(B1.t[R, :], t2.t[R, :], 1.0, pz.t[R, :], ALU.add, ALU.mult),
                     reads=[t2, pz], writes=[B1])
                c.op("dve", lambda e: e.scalar_tensor_tensor(g3[R, blk, cc * 512:(cc + 1) * 512], t1.t[R, :], 1.0, B1.t[R, :], ALU.add, ALU.mult),
                     reads=[t1, B1], writes=[gate])
                if side is not None:
                    next(side, None)
        if side is not None:
            for _ in side:
                pass

    def phase4(ntok, bp, y_dram, row0, side=None):
        uTv = fm(uT, ntok); oaTv = fm(oaT, ntok); obTv = fm(obT, ntok); mTv = fm(mT, ntok); x1Tv = fm(x1T, ntok)
        R = slice(0, bp)
        nblk = ntok // bp
        N = slice(0, ntok)
        for cc in range(2):
            wpa_ = load_w(wb_pa, cc * 512); wpb_ = load_w(wb_pb, cc * 512)
            wga = load_win(C_GA + cc * 512); wgb = load_win(C_GB + cc * 512)
            for g4 in range(4):
                gs = slice(g4 * 128, g4 * 128 + 128)
                pa = P(); pb = P(); pga = P(); pgb = P()
                mm8(pa.t[:, N], pa, lambda k: wpa_.t[:, k, gs], lambda k: oaTv[:, k, :], [wpa_, oaT])
                mm8(pb.t[:, N], pb, lambda k: wpb_.t[:, k, gs], lambda k: obTv[:, k, :], [wpb_, obT])
                mm8(pga.t[:, N], pga, lambda k: wga.t[:, k, gs], lambda k: uTv[:, k, :], [wga, uT])
                mm8(pgb.t[:, N], pgb, lambda k: wgb.t[:, k, gs], lambda k: uTv[:, k, :], [wgb, uT])
                ta = TF(); tb = TF(); m1 = TF(); m2 = TF()
                c.op("act", lambda e: e.activation(ta.t[:, N], pga.t[:, N], AF.Tanh, scale=0.5), reads=[pga], writes=[ta])
                c.op("act", lambda e: e.activation(tb.t[:, N], pgb.t[:, N], AF.Tanh, scale=0.5), reads=[pgb], writes=[tb])
                c.op("dve", lambda e: e.scalar_tensor_tensor(m1.t[:, N], ta.t[:, N], 1.0, pa.t[:, N], ALU.add, ALU.mult), reads=[ta, pa], writes=[m1])
                c.op("dve", lambda e: e.scalar_tensor_tensor(m2.t[:, N], tb.t[:, N], 1.0, pb.t[:, N], ALU.add, ALU.mult), reads=[tb, pb], writes=[m2])
                c.op("pool", lambda e: e.tensor_tensor(mTv[:, cc * 4 + g4, :], m1.t[:, N], m2.t[:, N], ALU.add), reads=[m1, m2], writes=[mT])
        mark('p4.y')
        wo = [load_w(wb_o, 0), load_w(wb_o, 512)]

        def emit_x1T(x1b, cols):
            b, bv = transposes(x1b, bp, None)
            c.op("act", lambda e: e.activation(x1Tv[:, :, cols], bv[:, :].rearrange("p (k t) -> p k t", k=8)[:, :, 0:bp], AF.Copy),
                 reads=[b], writes=[x1T])

        pend_x1 = None
        for blk in range(nblk):
            cols = slice(blk * bp, (blk + 1) * bp)
            x_ = xs[blk]; st = st4[blk]
            py = [P(), P()]
            for cc in range(2):
                mm8(py[cc].t[R, :], py[cc], lambda k: mTv[:, k, cols], lambda k: wo[cc].t[:, k, :], [mT, wo[cc]])
                junk = b1k[0]
                c.op("act", lambda e, cc=cc: e.activation(junk.t[R, 0:512], py[cc].t[R, :], AF.Square, accum_out=st.t[R, cc:cc + 1]),
                     reads=[py[cc]], writes=[junk, st], nofold=True)
            c.op("dve", lambda e: e.tensor_tensor(st.t[R, 2:3], st.t[R, 0:1], st.t[R, 1:2], ALU.add), reads=[st], writes=[st])
            c.op("dve", lambda e: e.tensor_scalar(st.t[R, 3:4], st.t[R, 2:3], 0.25 / 1024, EPS, ALU.mult, ALU.add), reads=[st], writes=[st])
            rsqrt(st.t[R, 4:5], st.t[R, 3:4], R, 1, [st])
            c.op("dve", lambda e: e.tensor_scalar(st.t[R, 5:6], st.t[R, 4:5], 0.5, None, ALU.mult), reads=[st], writes=[st])
            for cc in range(2):
                tt = TF()
                c.op("dve", lambda e, cc=cc: e.scalar_tensor_tensor(tt.t[R, :], py[cc].t[R, :], st.t[R, 5:6], gpost.t[R, cc * 512:(cc + 1) * 512], ALU.mult, ALU.mult),
                     reads=[py[cc], st, gpost], writes=[tt])
                c.op("pool", lambda e, cc=cc: e.tensor_tensor(x_.t[R, cc * 512:(cc + 1) * 512], tt.t[R, :], x_.t[R, cc * 512:(cc + 1) * 512], ALU.add),
                     reads=[tt, x_], writes=[x_])
            x1b = b1k[1 + blk % 2]
            c.op("act", lambda e: e.activation(x1b.t[R, :], x_.t[R, :], AF.Copy), reads=[x_], writes=[x1b])
            if pend_x1 is not None:
                emit_x1T(*pend_x1)
            pend_x1 = (x1b, cols)
        emit_x1T(*pend_x1)
        mark('p4.ple')
        wg = [load_w(wb_pg, 0), load_w(wb_pg, 512)]
        need_ple()
        for blk in range(nblk):
            cols = slice(blk * bp, (blk + 1) * bp)
            r0 = row0 + blk * bp
            x_ = xs[blk]
            yo_ = yo[blk % 2]
            for cc in range(2):
                pg = P(); pe_ = P()
                mm8(pg.t[R, :], pg, lambda k: x1Tv[:, k, cols], lambda k: wg[cc].t[:, k, :], [x1T, wg[cc]])
                mm8(pe_.t[R, :], pe_, lambda k: pT.t[:, k, cols], lambda k: wple.t[:, k, cc * 512:(cc + 1) * 512], [pT, wple], nk=2)
                tg = TF(); e2_ = TF()
                c.op("act", lambda e: e.activation(tg.t[R, :], pg.t[R, :], AF.Tanh, scale=0.5), reads=[pg], writes=[tg])
                c.op("dve", lambda e: e.scalar_tensor_tensor(e2_.t[R, :], tg.t[R, :], 1.0, pe_.t[R, :], ALU.add, ALU.mult), reads=[tg, pe_], writes=[e2_])
                c.op("dve", lambda e, cc=cc: e.scalar_tensor_tensor(yo_.t[R, cc * 512:(cc + 1) * 512], e2_.t[R, :], 0.5, x_.t[R, cc * 512:(cc + 1) * 512], ALU.mult, ALU.add),
                     reads=[e2_, x_], writes=[yo_])
            c.dma("act", y_dram[r0:r0 + bp, :], yo_.t[R, :], dyo[blk % 2], reads=[yo_])
            if side is not None and blk >= 1:
                sv = c.redirect; c.redirect = False
                next(side, None)
                c.redirect = sv
        if side is not None:
            sv = c.redirect; c.redirect = False
            for _ in side:
                pass
            c.redirect = sv

    def mark(lbl):
        MARKS.append((lbl, c.nins['pe']))

    def prompt_tile(ti):
        uT3 = v3(uT, 8)
        mark('t%d.p0' % ti)
        phase0(NT, 128, xp, pp, ti * NT)
        cast_stage(1)
        if STOP <= 1:
            return
        mark('t%d.p1' % ti)
        bstate["list"] = list(range(8))
        qt3 = v3(qt, 8); kt3 = v3(kt, 8)
        for hg in range(2):
            wq = load_win(C_AQ + hg * 512)
            wf = load_win(C_AF + hg * 512)
            for hh in range(4):
                h = hg * 4 + hh
                pq = P(); pf2 = P()
                mm8(pq.t[:, :], pq, lambda k: wq.t[:, k, hh * 128:(hh + 1) * 128], lambda k: uT3[:, k, :], [wq, uT])
                mm8(pf2.t[:, :], pf2, lambda k: wf.t[:, k, hh * 128:(hh + 1) * 128], lambda k: uT3[:, k, :], [wf, uT])
                tq = TF(); A = TF(); tf = TF(); fg = TF(); Pc = TF(); kk = TF()
                c.op("act", lambda e: e.activation(tq.t[:, :], pq.t[:, :], AF.Tanh, scale=0.5), reads=[pq], writes=[tq])
                c.op("dve", lambda e: e.scalar_tensor_tensor(A.t[:, :], tq.t[:, :], 1.0, pq.t[:, :], ALU.add, ALU.mult),
                     reads=[tq, pq], writes=[A])
                c.op("act", lambda e: e.activation(tf.t[:, :], pf2.t[:, :], AF.Tanh, scale=0.5), reads=[pf2], writes=[tf])
                c.op("act", lambda e: e.activation(fg.t[:, :], tf.t[:, :], AF.Identity, scale=c1.t[:, h:h + 1], bias=c0.t[:, h:h + 1]),
                     reads=[tf, c1, c0], writes=[fg])
                c.op("dve", lambda e: e.tensor_tensor_scan(Pc.t[:, :], scanm.t[:, :], fg.t[:, :], 1.0, ALU.max, ALU.mult),
                     reads=[scanm, fg], writes=[Pc])
                pe1 = "dve" if ti == 0 else "pool"
                c.op(pe1, lambda e: e.tensor_tensor(qt3[:, h, :], A.t[:, :], Pc.t[:, :], ALU.mult),
                     reads=[A, Pc], writes=[qt])
                c.op("dve", lambda e: e.tensor_copy(Pl.t[:, h, :], Pc.t[:, :].rearrange("p (c t) -> p c t", t=64)[:, :, 63]),
                     reads=[Pc], writes=[Pl])
                c.op("act", lambda e: e.activation(kk.t[:, :], tf.t[:, :], AF.Identity, scale=nc1.t[:, h:h + 1], bias=c1.t[:, h:h + 1]),
                     reads=[tf, nc1, c1], writes=[kk])
                if USE_POOL_DIV:
                    rP = TF()
                    c.op("pool", lambda e: e.tensor_tensor(rP.t[:, :], Pc.t[:, :], bc(negone.t[:, 0], 512), ALU.pow), reads=[Pc, negone], writes=[rP])
                    c.op(pe1, lambda e: e.tensor_tensor(kt3[:, h, :], kk.t[:, :], rP.t[:, :], ALU.mult),
                         reads=[kk, rP], writes=[kt])
                else:
                    rP = TF()
                    c.op("dve", lambda e: e.reciprocal(rP.t[:, :], Pc.t[:, :]), reads=[Pc], writes=[rP])
                    c.op(pe1, lambda e: e.tensor_tensor(kt3[:, h, :], kk.t[:, :], rP.t[:, :], ALU.mult),
                         reads=[kk, rP], writes=[kt])
        mark('t%d.ktok' % ti)
        ktok3 = v3(ktok, 4)

        def ktok_gen():
            for blk in range(4):
                b = P(); bv = bfv(b)
                for h in range(8):
                    c.op("pe", lambda e, h=h: e.transpose(bv[:, h * 128:(h + 1) * 128], kt3[:, h, blk * 128:(blk + 1) * 128], ident.t[:, :]),
                         reads=[kt, ident], writes=[b], signal=(h == 7))
                c.op("act", lambda e: e.activation(ktok3[:, blk, :], bv[:, :], AF.Copy), reads=[b], writes=[ktok])
                yield

        late_setup()
        mark('t%d.vgate' % ti)
        cast_stage(2)
        v3v = v3(vv, 4); g3 = v3(gate, 4)
        tokmajor_v_gate(NT, 128, side=ktok_gen())
        bstate["list"] = None
        cast_stage(3)
        if STOP <= 2:
            return
        mark('t%d.p2' % ti)
        oaT3 = v3(oaT, 8)

        def s_mm(blk, ci):
            rows = slice(ci * 64, ci * 64 + 64)
            out = []
            for half in range(2):
                ps_ = P()
                for hh in range(4):
                    h = half * 4 + hh
                    c.op("pe", lambda e, h=h, hh=hh: e.matmul(ps_.t[:, hh * 128:(hh + 1) * 128],
                                                             ktok3[rows, blk, h * 128:(h + 1) * 128],
                                                             v3v[rows, blk, h * 128:(h + 1) * 128], start=True, stop=True),
                         reads=[ktok, vv], writes=[ps_], signal=(hh == 3))
                out.append(ps_)
            return out

        def s_chain(blk, ci, pss):
            ch = blk * 2 + ci
            s_old = Sbf[sstate["i"] % 3]
            sstate["i"] += 1
            s_new = Sbf[sstate["i"] % 3]
            for half in range(2):
                ps_ = pss[half]
                hs = slice(half * 4, half * 4 + 4)
                c.op("dve", lambda e: e.tensor_tensor(S.t[:, hs, :], ps_.t[:, :].rearrange("p (h d) -> p h d", h=4), S.t[:, hs, :], ALU.add),
                     reads=[ps_, S2[half]], writes=[S2[half]])
                c.op("dve", lambda e: e.tensor_tensor(S.t[:, hs, :], S.t[:, hs, :], bc(Pl.t[:, hs, ch], 128), ALU.mult),
                     reads=[S2[half], Pl], writes=[S2[half]])
                c.op("act", lambda e: e.activation(s_new.t[:, hs, :], S.t[:, hs, :], AF.Copy), reads=[S2[half]], writes=[s_new])
            return s_old, s_new

        def emit_oaT(oa, cols):
            b, bv = transposes(oa, 128, None)
            c.op("act", lambda e: e.activation(oaT3[:, :, cols], bv[:, :].rearrange("p (k t) -> p k t", k=8), AF.Copy),
                 reads=[b], writes=[oaT])

        bstate["list"] = [0, 1, 2, 3, 4, 7]
        pend_oa = None
        for blk in range(4):
            cols = slice(blk * 128, blk * 128 + 128)
            g2 = b1k[0]
            c.op("pool", lambda e: e.tensor_tensor(g2.t[:, :], g3[:, blk, :], gn125.t[:, :], ALU.mult),
                 reads=[gate, gn125], writes=[g2])
            aT = attT[blk % 2]
            for half in range(2):
                pa = P()
                for hh in range(4):
                    h = half * 4 + hh
                    c.op("pe", lambda e, h=h, hh=hh: e.matmul(pa.t[:, hh * 128:(hh + 1) * 128], kt3[:, h, cols], qt3[:, h, cols],
                                                             start=True, stop=True),
                         reads=[kt, qt], writes=[pa], signal=(hh == 3))
                c.op("dve", lambda e: e.tensor_tensor(aT.t[:, half * 4:half * 4 + 4, :],
                                                      pa.t[:, :].rearrange("p (h t) -> p h t", h=4),
                                                      bcmid(mask64.t[:, :], 4), ALU.mult),
                     reads=[pa, mask64], writes=[aT])
            pssA = s_mm(blk, 0)
            pssB = s_mm(blk, 1)
            s0, s1 = s_chain(blk, 0, pssA)
            s_chain(blk, 1, pssB)
            po = [banks[5], banks[6]]
            for h in range(8):
                pb = po[h // 4]
                oc = slice((h % 4) * 128, (h % 4) * 128 + 128)
                c.op("pe", lambda e, h=h: e.matmul(pb.t[:, oc], aT.t[:, h, :], v3v[:, blk, h * 128:(h + 1) * 128], start=True, stop=False),
                     reads=[aT, vv], writes=[pb], signal=False)
                c.op("pe", lambda e, h=h: e.matmul(pb.t[0:64, oc], qt3[:, h, blk * 128:blk * 128 + 64], s0.t[:, h, :], start=False, stop=True),
                     reads=[qt, s0], writes=[pb], signal=False)
                c.op("pe", lambda e, h=h: e.matmul(pb.t[64:128, oc], qt3[:, h, blk * 128 + 64:blk * 128 + 128], s1.t[:, h, :], start=False, stop=True),
                     reads=[qt, s1], writes=[pb], signal=(h % 4 == 3))
            sq = F1K(); st = st4[blk]
            for half in range(2):
                c.op("act", lambda e, half=half: e.activation(sq.t[:, half * 512:(half + 1) * 512], po[half].t[:, :], AF.Square),
                     reads=[po[half]], writes=[sq])
            c.op("dve", lambda e: e.tensor_reduce(st.t[:, 0:8], sq.t[:, :].rearrange("p (h d) -> p h d", h=8), AX.X, ALU.add),
                 reads=[sq], writes=[st])
            c.op("dve", lambda e: e.tensor_scalar(st.t[:, 0:8], st.t[:, 0:8], 0.25 / 128, EPS, ALU.mult, ALU.add), reads=[st], writes=[st])
            rsqrt(st.t[:, 8:16], st.t[:, 0:8], slice(0, 128), 8, [st])
            ot = F1K()
            for half in range(2):
                c.op("dve", lambda e, half=half: e.tensor_tensor(ot.t[:, half * 512:(half + 1) * 512].rearrange("p (h d) -> p h d", h=4),
                                                                po[half].t[:, :].rearrange("p (h d) -> p h d", h=4),
                                                                bc(st.t[:, 8 + half * 4:12 + half * 4], 128), ALU.mult),
                     reads=[po[half], st], writes=[ot])
            oa = b1k[1 + blk % 2]
            c.op("pool", lambda e: e.tensor_tensor(oa.t[:, :], ot.t[:, :], g2.t[:, :], ALU.mult), reads=[ot, g2], writes=[oa])
            if pend_oa is not None:
                emit_oaT(*pend_oa)
            pend_oa = (oa, cols)
        emit_oaT(*pend_oa)
        bstate["list"] = None

        if STOP <= 3:
            return
        mark('t%d.p3in' % ti)
        c.redirect = False
        need_kv()
        bq3 = v3(bqT, 8)
        for cc in range(2):
            wq = load_win(C_BQ + cc * 512)
            for g4 in range(4):
                pq = P()
                mm8(pq.t[:, :], pq, lambda k: wq.t[:, k, g4 * 128:(g4 + 1) * 128], lambda k: uT3[:, k, :], [wq, uT])
                eng = "act" if g4 % 2 == 0 else "dve"
                if eng == "act":
                    c.op("act", lambda e: e.activation(bq3[:, cc * 4 + g4, :], pq.t[:, :], AF.Copy), reads=[pq], writes=[bqT])
                else:
                    c.op("dve", lambda e: e.tensor_copy(bq3[:, cc * 4 + g4, :], pq.t[:, :]), reads=[pq], writes=[bqT])
        if ti > 0:
            c.op("pool", lambda e: e.tensor_copy(kT2.t[:, :, 0:128], kT2.t[:, :, 512:640]), reads=[kT2], writes=[kT2])
            c.op("pool", lambda e: e.tensor_copy(vaug.t[:, 0, :, :], vaug.t[:, 4, :, :]), reads=[vaug], writes=[vaug])
        for g in range(2):
            pk = P()
            mm8(pk.t[:, :], pk, lambda k: wk2.t[:, k, g, :, :], lambda k: uT3[:, k, :], [wk2, uT])
            c.op("dve", lambda e: e.tensor_copy(kT2.t[:, g, 128:640], pk.t[:, :]), reads=[pk], writes=[kT2])
        for blk in range(4):
            pv = P()
            mm8(pv.t[:, 0:128], pv, lambda k: uT3[:, k, blk * 128:(blk + 1) * 128], lambda k: wbv.t[:, k, :], [wbv, uT])
            c.op("act", lambda e: e.activation(vaug.t[:, blk + 1, :, 0:64], pv.t[:, 0:128].rearrange("p (g d) -> p g d", g=2), AF.Copy),
                 reads=[pv], writes=[vaug])
        zs3 = v3(zs2, 4)
        for cc in range(2):
            wz = load_win(C_BZ + cc * 512)
            for blk in range(4):
                pz = P()
                mm8(pz.t[:, :], pz, lambda k: uT3[:, k, blk * 128:(blk + 1) * 128], lambda k: wz.t[:, k, :], [wz, uT])
                t2 = TF()
                c.op("act", lambda e: e.activation(t2.t[:, :], pz.t[:, :], AF.Tanh, scale=0.5), reads=[pz], writes=[t2])
                c.op("dve", lambda e: e.scalar_tensor_tensor(zs3[:, blk, cc * 512:(cc + 1) * 512], t2.t[:, :], 1.0, pz.t[:, :], ALU.add, ALU.mult),
                     reads=[t2, pz], writes=[zs2])
        obT3 = v3(obT, 8)
        mark('t%d.p3att' % ti)

        def emit_obT(ob, cols_):
            b, bv = transposes(ob, 128, None)
            c.op("act", lambda e: e.activation(obT3[:, :, cols_], bv[:, :].rearrange("p (k t) -> p k t", k=8), AF.Copy),
                 reads=[b], writes=[obT])

        pend_ob = [None]
        for blk in range(4):
            gb = ti * 4 + blk
            cols = slice(blk * 128, blk * 128 + 128)
            pvb = [banks[5], banks[6], banks[7]]
            def scores(pgp):
                pss = [P(), P()]
                nmm = 2 if gb == 0 else 4
                for e2 in range(2):
                    rows = slice(e2 * 64, e2 * 64 + 64)
                    ps_ = pss[e2]
                    i = 0
                    for j in range(2):
                        hq = 4 * pgp + 2 * j + e2; g = hq // 8; pair = hq // 2
                        i += 1
                        c.op("pe", lambda e, rows=rows, g=g, j=j, pair=pair, ps_=ps_: e.matmul(
                            ps_.t[:, j * 256:j * 256 + 128], kT2.t[rows, g, 128 + blk * 128:256 + blk * 128],
                            bq3[rows, pair, cols], start=True, stop=True),
                             reads=[kT2, bqT], writes=[ps_], signal=(i == nmm))
                        if gb > 0:
                            i += 1
                            c.op("pe", lambda e, rows=rows, g=g, j=j, pair=pair, ps_=ps_: e.matmul(
                                ps_.t[:, j * 256 + 128:j * 256 + 256], kT2.t[rows, g, blk * 128:128 + blk * 128],
                                bq3[rows, pair, cols], start=True, stop=True),
                                 reads=[kT2, bqT], writes=[ps_], signal=(i == nmm))
                return pss

            def softmax_num(pgp, pss):
                pts = []
                for e2 in range(2):
                    ps_ = pss[e2]
                    eP = TF()
                    pt = PTt[ptstate["i"] % 4]; ptstate["i"] += 1
                    h0 = 4 * pgp + e2
                    Esl = Et.t[:, h0:h0 + 3:2, :]
                    if gb > 0:
                        c.op("act", lambda e: e.activation(eP.t[:, :], ps_.t[:, :], AF.Exp, scale=0.125), reads=[ps_], writes=[eP])
                        eng = "dve"
                        c.op(eng, lambda e: e.tensor_tensor(pt.t[:, :].rearrange("p (j x) -> p j x", j=2),
                                                            eP.t[:, :].rearrange("p (j x) -> p j x", j=2), Esl, ALU.mult),
                             reads=[eP, Et], writes=[pt])
                    else:
                        c.op("act", lambda e: e.activation(eP.t[:, :].rearrange("p (h s t) -> p h s t", h=2, s=2)[:, :, 0, :],
                                                           ps_.t[:, :].rearrange("p (h s t) -> p h s t", h=2, s=2)[:, :, 0, :],
                                                           AF.Exp, scale=0.125), reads=[ps_], writes=[eP])
                        c.op("dve", lambda e: e.tensor_tensor(pt.t[:, :].rearrange("p (h s t) -> p h s t", h=2, s=2)[:, :, 0, :],
                                                              eP.t[:, :].rearrange("p (h s t) -> p h s t", h=2, s=2)[:, :, 0, :],
                                                              Esl[:, :, 0:128], ALU.mult),
                             reads=[eP, Et], writes=[pt])
                    pts.append(pt)
                return pts

            def pv_mm(pgp, pts):
                for e2 in range(2):
                    pt = pts[e2]
                    for j in range(2):
                        hq = 4 * pgp + 2 * j + e2; g = hq // 8
                        pb = pvb[hq // 7]; off = (hq % 7) * 65
                        c.op("pe", lambda e, j=j, g=g, off=off, pb=pb, pt=pt: e.matmul(
                            pb.t[:, off:off + 65], pt.t[:, j * 256:j * 256 + 128], vaug.t[:, blk + 1, g, :], start=True, stop=(gb == 0)),
                             reads=[pt, vaug], writes=[pb], signal=(gb == 0))
                        if gb > 0:
                            c.op("pe", lambda e, j=j, g=g, off=off, pb=pb, pt=pt: e.matmul(
                                pb.t[:, off:off + 65], pt.t[:, j * 256 + 128:j * 256 + 256], vaug.t[:, blk, g, :], start=False, stop=True),
                                 reads=[pt, vaug], writes=[pb], signal=True)

            pend = None
            for pgp in range(4):
                pss = scores(pgp)
                if pgp == 2 and pend_ob[0] is not None:
                    emit_obT(*pend_ob[0])
                    pend_ob[0] = None
                if pend is not None:
                    pv_mm(*pend)
                pts = softmax_num(pgp, pss)
                pend = (pgp, pts)
            pv_mm(*pend)
            obt = F1K()
            for j in range(3):
                nh = 7 if j < 2 else 2
                pv3 = pvb[j].t[:, 0:nh * 65].rearrange("p (h d) -> p h d", d=65)
                c.op("dve", lambda e, j=j, nh=nh, pv3=pv3: e.tensor_tensor(den.t[:, 7 * j:7 * j + nh], pv3[:, :, 64], esink.t[:, 7 * j:7 * j + nh], ALU.add),
                     reads=[pvb[j], esink], writes=[den])
            c.op("dve", lambda e: e.tensor_scalar(den.t[:, :], den.t[:, :], 2.0, None, ALU.mult), reads=[den], writes=[den])
            c.op("dve", lambda e: e.reciprocal(rden.t[:, :], den.t[:, :]), reads=[den], writes=[rden])
            for j in range(3):
                nh = 7 if j < 2 else 2
                pv3 = pvb[j].t[:, 0:nh * 65].rearrange("p (h d) -> p h d", d=65)
                c.op("dve", lambda e, j=j, nh=nh, pv3=pv3: e.tensor_tensor(obt.t[:, 7 * j * 64:(7 * j + nh) * 64].rearrange("p (h d) -> p h d", d=64),
                                                                          pv3[:, :, 0:64], bc(rden.t[:, 7 * j:7 * j + nh], 64), ALU.mult),
                     reads=[pvb[j], rden], writes=[obt])
            ob = b1k[1 + blk % 2]
            c.op("dve", lambda e: e.tensor_tensor(ob.t[:, :], obt.t[:, :], zs3[:, blk, :], ALU.mult), reads=[obt, zs2], writes=[ob])
            pend_ob[0] = (ob, cols)
        emit_obT(*pend_ob[0])

        if STOP <= 4:
            return
        mark('t%d.p4' % ti)
        phase4(NT, 128, yp, ti * NT, side=None)


    def sample_path():
        late_setup()
        NS = 64
        sx_list = [xs[1], xs[2], xs[3]]
        sx_ds = [dxs[1], dxs[2], dxs[3]]

        def issue_in(bi):
            Sxi = sx_list[bi % 3]
            c.dma("sp", Sxi.t[:, :].rearrange("p (h d) -> p h d", h=8), st0[bi].rearrange("h p v -> p h v"), sx_ds[bi % 3], writes=[Sxi])

        issue_in(0); issue_in(1)
        mark('sample')
        R64 = slice(0, 64)
        phase0(NS, 64, xsm, psm, 0)
        uTs = fm(uT, NS)

        def fm_proj(col0):
            bk_ = P()
            for hg in range(2):
                w = load_win(col0 + hg * 512)
                for hh in range(4):
                    h = hg * 4 + hh
                    mm8(bk_.t[:, h * 64:(h + 1) * 64], bk_, lambda k: w.t[:, k, hh * 128:(hh + 1) * 128], lambda k: uTs[:, k, :], [w, uT])
            return bk_

        def h3(a):
            return a.rearrange("p (h t) -> p h t", h=8)

        Pcs = S.t[:, 0:4, :].rearrange("p a b -> p (a b)")
        gT2 = S.t[:, 4:8, :].rearrange("p a b -> p (a b)")
        qts = qt.t[:, 0:512]; kts = qt.t[:, 512:1024]; khs = qt.t[:, 1024:1536]
        khtok = qt.t[0:64, 2048:3072]
        pq = fm_proj(C_AQ); pfb = fm_proj(C_AF)
        tq = TF(); A = TF(); tf = TF(); fg = TF(); kk = TF(); rP = TF(); m4 = TF()
        c.op("act", lambda e: e.activation(tq.t[:, :], pq.t[:, :], AF.Tanh, scale=0.5), reads=[pq], writes=[tq])
        c.op("dve", lambda e: e.scalar_tensor_tensor(A.t[:, :], tq.t[:, :], 1.0, pq.t[:, :], ALU.add, ALU.mult), reads=[tq, pq], writes=[A])
        c.op("act", lambda e: e.activation(tf.t[:, :], pfb.t[:, :], AF.Tanh, scale=0.5), reads=[pfb], writes=[tf])
        c.op("dve", lambda e: e.tensor_tensor(h3(fg.t[:, :]), h3(tf.t[:, :]), bc(c1.t[:, 0:8], 64), ALU.mult), reads=[tf, c1], writes=[fg])
        c.op("dve", lambda e: e.tensor_tensor(h3(fg.t[:, :]), h3(fg.t[:, :]), bc(c0.t[:, 0:8], 64), ALU.add), reads=[fg, c0], writes=[fg])
        c.op("pool", lambda e: e.memset(m4.t[:, :], 0.0), writes=[m4])
        c.op("pool", lambda e: e.memset(m4.t[:, :].rearrange("p (a t) -> p a t", t=4)[:, :, 0], 1.0), writes=[m4])
        c.op("dve", lambda e: e.tensor_tensor_scan(Pcs, m4.t[:, :], fg.t[:, :], 1.0, ALU.max, ALU.mult), reads=[m4, fg], writes=S2)
        c.op("pool", lambda e: e.tensor_tensor(qts, A.t[:, :], Pcs, ALU.mult), reads=[A] + S2, writes=[qt])
        c.op("pool", lambda e: e.tensor_tensor(h3(kk.t[:, :]), h3(tf.t[:, :]), bc(nc1.t[:, 0:8], 64), ALU.mult), reads=[tf, nc1], writes=[kk])
        c.op("pool", lambda e: e.tensor_tensor(h3(kk.t[:, :]), h3(kk.t[:, :]), bc(c1.t[:, 0:8], 64), ALU.add), reads=[kk, c1], writes=[kk])
        c.op("dve", lambda e: e.reciprocal(rP.t[:, :], Pcs), reads=S2, writes=[rP])
        c.op("pool", lambda e: e.tensor_tensor(kts, kk.t[:, :], rP.t[:, :], ALU.mult), reads=[kk, rP], writes=[qt])
        P4all = bc(Pcs.rearrange("p (a t) -> p a t", t=4)[:, :, 3], 4)
        c.op("dve", lambda e: e.tensor_tensor(khs.rearrange("p (a t) -> p a t", t=4), kts.rearrange("p (a t) -> p a t", t=4), P4all, ALU.mult),
             reads=[qt] + S2, writes=[qt])
        b = P(); bv = bfv(b)
        for h in range(8):
            c.op("pe", lambda e, h=h: e.transpose(bv[0:64, h * 128:(h + 1) * 128], khs[:, h * 64:(h + 1) * 64], ident.t[:, :]),
                 reads=[qt, ident], writes=[b], signal=(h == 7))
        c.op("act", lambda e: e.activation(khtok, bv[0:64, :], AF.Copy), reads=[b], writes=[qt])
        pg = fm_proj(C_AOG); pz = fm_proj(C_AZ)
        t1 = TF(); t2 = TF(); B1 = TF(); gT = TF()
        c.op("act", lambda e: e.activation(t1.t[:, :], pg.t[:, :], AF.Tanh, scale=0.5), reads=[pg], writes=[t1])
        c.op("act", lambda e: e.activation(t2.t[:, :], pz.t[:, :], AF.Tanh, scale=0.5), reads=[pz], writes=[t2])
        c.op("dve", lambda e: e.scalar_tensor_tensor(B1.t[:, :], t2.t[:, :], 1.0, pz.t[:, :], ALU.add, ALU.mult), reads=[t2, pz], writes=[B1])
        c.op("dve", lambda e: e.scalar_tensor_tensor(gT.t[:, :], t1.t[:, :], 1.0, B1.t[:, :], ALU.add, ALU.mult), reads=[t1, B1], writes=[gT])
        c.op("pool", lambda e: e.tensor_tensor(h3(gT2), h3(gT.t[:, :]), bc(gnT.t[:, 0:8], 64), ALU.mult), reads=[gT, gnT], writes=S2)
        tokmajor_v_gate(NS, 64)
        v3v = v3(vv, 4)
        vs_ = v3v[0:64, 0, :]
        pa = P()
        for h in range(8):
            c.op("pe", lambda e, h=h: e.matmul(pa.t[0:64, h * 64:(h + 1) * 64], kts[:, h * 64:(h + 1) * 64], qts[:, h * 64:(h + 1) * 64], start=True, stop=True),
                 reads=[qt], writes=[pa], signal=(h == 7))
        aT = attT[0]
        c.op("pool", lambda e: e.memset(aT.t[64:128, :, :], 0.0), writes=[aT])
        c.op("dve", lambda e: e.tensor_tensor(aT.t[0:64, :, 0:64], pa.t[0:64, :].rearrange("p (h t) -> p h t", h=8),
                                              bcmid(mask4.t[:, :], 8), ALU.mult), reads=[pa, mask4], writes=[aT])
        def swa_gen():
            need_kv()
            bqTs = kt.t[:, 0:512]; kTn = kt.t[:, 512:640]; zsT = kt.t[:, 1024:1536]
            KcT = ktok
            KcT4 = KcT.t[:, :].rearrange("p (b g s) -> p b g s", b=16, g=2)
            vc = gate.t[:, 0:2048].rearrange("p (b f) -> p b f", b=16)
            vnew = gate.t[:, 2048:2176]
            pbq = P()
            for cc in range(2):
                wq = load_win(C_BQ + cc * 512)
                for g4 in range(4):
                    cq = cc * 4 + g4
                    mm8(pbq.t[:, cq * 64:(cq + 1) * 64], pbq, lambda k: wq.t[:, k, g4 * 128:(g4 + 1) * 128], lambda k: uTs[:, k, :], [wq, uT])
            c.op("act", lambda e: e.activation(bqTs, pbq.t[:, :], AF.Copy), reads=[pbq], writes=[kt])
            yield
            pkn = P()
            for g in range(2):
                mm8(pkn.t[:, g * 64:(g + 1) * 64], pkn, lambda k: wk2.t[:, k, g, :, :], lambda k: uTs[:, k, :], [wk2, uT])
            c.op("dve", lambda e: e.tensor_copy(kTn, pkn.t[:, 0:128]), reads=[pkn], writes=[kt])
            yield
            knv = TF()
            pvn = P()
            mm8(pvn.t[0:64, 0:128], pvn, lambda k: uTs[:, k, :], lambda k: wbv.t[:, k, :], [wbv, uT])
            c.op("pool", lambda e: e.memset(vnew, 0.0), writes=[gate])
            c.op("act", lambda e: e.activation(vnew[0:64, :], pvn.t[0:64, 0:128], AF.Copy), reads=[pvn], writes=[gate])
            c.op("dve", lambda e: e.tensor_copy(knv.t[0:64, 0:128], pvn.t[0:64, 0:128]), reads=[pvn], writes=[knv])
            pkk = P()
            for g in range(2):
                mm8(pkk.t[0:64, g * 64:(g + 1) * 64], pkk, lambda k: uTs[:, k, :], lambda k: wk2.t[:, k, g, 0, :], [wk2, uT])
            c.op("dve", lambda e: e.tensor_copy(knv.t[0:64, 128:256], pkk.t[0:64, 0:128]), reads=[pkk], writes=[knv])
            for b_ in range(16):
                c.dma("sp", vs_o[b_, 124:128, :], knv.t[b_ * 4:(b_ + 1) * 4, 0:128], d_misc, reads=[knv])
                c.dma("sp", ks_o[b_, 124:128, :], knv.t[b_ * 4:(b_ + 1) * 4, 128:256], d_misc, reads=[knv])
            c.dma("sp", ks_o[:, 0:124, :], ck[:, 4:128, :], d_misc)
            c.dma("sp", vs_o[:, 0:124, :], cv[:, 4:128, :], d_misc)
            yield
            pbz = P()
            for cc in range(2):
                wz = load_win(C_BZ + cc * 512)
                for g4 in range(4):
                    cq = cc * 4 + g4
                    mm8(pbz.t[:, cq * 64:(cq + 1) * 64], pbz, lambda k: wz.t[:, k, g4 * 128:(g4 + 1) * 128], lambda k: uTs[:, k, :], [wz, uT])
            t2 = TF()
            c.op("act", lambda e: e.activation(t2.t[:, :], pbz.t[:, :], AF.Tanh, scale=0.5), reads=[pbz], writes=[t2])
            c.op("dve", lambda e: e.scalar_tensor_tensor(zsT, t2.t[:, :], 1.0, pbz.t[:, :], ALU.add, ALU.mult), reads=[t2, pbz], writes=[kt])
            yield
            for half in range(2):
                Kf = f1k[half]
                Kf3 = Kf.t[:, :].rearrange("p (b f) -> p b f", b=8)
                c.dma("sp", Kf3, ck[half * 8:(half + 1) * 8, :, :].rearrange("b s f -> s b f"), dyo[half], writes=[Kf])
                for q4 in range(2):
                    kcd = B1K()
                    kcd5 = kcd.t[:, :].rearrange("p (b g u d) -> p b g u d", b=4, g=2, u=2)
                    for dup in range(2):
                        c.op("pool" if dup == 0 else "act",
                             (lambda e, dup=dup: e.tensor_copy(kcd5[:, :, :, dup, :], Kf3[:, q4 * 4:q4 * 4 + 4, :].rearrange("p b (g d) -> p b g d", g=2)))
                             if dup == 0 else
                             (lambda e, dup=dup: e.activation(kcd5[:, :, :, dup, :], Kf3[:, q4 * 4:q4 * 4 + 4, :].rearrange("p b (g d) -> p b g d", g=2), AF.Copy)),
                             reads=[Kf], writes=[kcd])
                    bkT = P(); bvT = bfv(bkT)
                    for bb in range(4):
                        for g in range(2):
                            i8 = bb * 2 + g
                            c.op("pe", lambda e, bb=bb, g=g, i8=i8: e.transpose(bvT[:, i8 * 128:(i8 + 1) * 128],
                                                                               kcd.t[:, (bb * 2 + g) * 128:(bb * 2 + g + 1) * 128], ident.t[:, :]),
                                 reads=[kcd, ident], writes=[bkT], signal=(i8 == 7))
                    b0 = half * 8 + q4 * 4
                    c.op("dve", lambda e: e.tensor_copy(KcT.t[:, b0 * 256:(b0 + 4) * 256], bvT[:, :]), reads=[bkT], writes=[KcT])
                    yield
            for half in range(2):
                Vf = f1k[half]
                Vf3 = Vf.t[:, :].rearrange("p (b f) -> p b f", b=8)
                c.dma("sp", Vf3, cv[half * 8:(half + 1) * 8, :, :].rearrange("b s f -> s b f"), dyo[half], writes=[Vf])
                c.op("act", lambda e: e.activation(vc[:, half * 8:(half + 1) * 8, :], Vf3, AF.Copy), reads=[Vf], writes=[gate])
                yield
            Enb = b1k[1]; PNb = b1k[0]
            c.op("pool", lambda e: e.memset(PNb.t[:, :], 0.0), writes=[PNb])
            for par in range(2):
                c.op("dve", lambda e, par=par: e.tensor_tensor(Enb.t[0:64, par * 512:(par + 1) * 512].rearrange("p (b h t) -> p b h t", b=16, h=8),
                                                              rawap(Rrep.t, par * 4 + 3, [[64, 64], [0, 16], [8, 8], [-1, 4]]),
                                                              rawap(bmask.t, 0, [[16, 64], [1, 16], [0, 8], [0, 4]]), ALU.mult),
                     reads=[Rrep, bmask], writes=[Enb])
            scC = [banks[6], banks[7]]
            scN = [banks[3], banks[4]]
            for par in range(2):
                rows = slice(par * 64, par * 64 + 64)
                n = 0
                for b_ in range(16):
                    for he in range(8):
                        hq = 2 * he + par; g = hq // 8
                        cs = slice(b_ * 32 + he * 4, b_ * 32 + he * 4 + 4)
                        n += 1
                        c.op("pe", lambda e, b_=b_, he=he, g=g, cs=cs: e.matmul(scC[par].t[:, cs], KcT4[rows, b_, g, :], bqTs[rows, he * 64 + b_ * 4:he * 64 + b_ * 4 + 4],
                                                                               start=True, stop=True), reads=[KcT, kt], writes=[scC[par]], signal=(n % 32 == 0))
                    if b_ % 4 == 3 and he == 7:
                        yield
                n = 0
                for b_ in range(16):
                    for he in range(8):
                        hq = 2 * he + par; g = hq // 8
                        cs = slice(b_ * 32 + he * 4, b_ * 32 + he * 4 + 4)
                        n += 1
                        c.op("pe", lambda e, b_=b_, he=he, g=g, cs=cs: e.matmul(scN[par].t[0:64, cs], kTn[rows, g * 64:(g + 1) * 64], bqTs[rows, he * 64 + b_ * 4:he * 64 + b_ * 4 + 4],
                                                                               start=True, stop=True), reads=[kt], writes=[scN[par]], signal=(n % 32 == 0))
                    if b_ % 4 == 3 and he == 7:
                        yield
            PC = [PTt[1], PTt[2]]
            for par in range(2):
                eC = TF(); eN = TF()
                c.op("act", lambda e: e.activation(eC.t[:, :], scC[par].t[:, :], AF.Exp, scale=0.125), reads=[scC[par]], writes=[eC])
                c.op("act", lambda e: e.activation(eN.t[0:64, :], scN[par].t[0:64, :], AF.Exp, scale=0.125), reads=[scN[par]], writes=[eN])
                c.op("dve", lambda e, par=par: e.tensor_tensor(PC[par].t[:, :].rearrange("p (b h t) -> p b h t", b=16, h=8),
                                                              eC.t[:, :].rearrange("p (b h t) -> p b h t", b=16, h=8),
                                                              rawap(Et.t, par * 256 + 128, [[4096, 128], [0, 16], [512, 8], [1, 4]]), ALU.mult),
                     reads=[eC, Et], writes=[PC[par]])
                c.op("pool", lambda e, par=par: e.tensor_tensor(PNb.t[0:64, par * 512:(par + 1) * 512], eN.t[0:64, :], Enb.t[0:64, par * 512:(par + 1) * 512], ALU.mult),
                     reads=[eN, Enb], writes=[PNb])
            pvT = banks[6]; pden = banks[7]
            for par in range(2):
                orow = slice(par * 64, par * 64 + 64)
                pc2 = PC[par].t[:, :]
                pn2 = PNb.t[:, par * 512:(par + 1) * 512]
                ov2 = pvT.t[orow, :]
                for b_ in range(16):
                    for g in range(2):
                        hs = slice(b_ * 32 + g * 16, b_ * 32 + g * 16 + 16)
                        c.op("pe", lambda e, b_=b_, g=g, hs=hs: e.matmul(ov2[:, hs], vc[:, b_, g * 64:(g + 1) * 64], pc2[:, hs], start=True, stop=False),
                             reads=[gate, PC[par]], writes=[pvT], signal=False)
                        c.op("pe", lambda e, b_=b_, g=g, hs=hs: e.matmul(ov2[:, hs], vnew[:, g * 64:(g + 1) * 64], pn2[:, hs], start=False, stop=True),
                             reads=[gate, PNb], writes=[pvT], signal=(b_ % 4 == 3 and g == 1))
                    if b_ % 4 == 3:
                        yield
                c.op("pe", lambda e: e.matmul(pden.t[orow, :], ones128.t[:, 0:64], PC[par].t[:, :], start=True, stop=False),
                     reads=[ones128, PC[par]], writes=[pden], signal=False)
                c.op("pe", lambda e: e.matmul(pden.t[orow, :], ones128.t[:, 0:64], PNb.t[:, par * 512:(par + 1) * 512], start=False, stop=True),
                     reads=[ones128, PNb], writes=[pden], signal=True)
            d_ = TF(); rd = TF(); o1 = TF()
            c.op("dve", lambda e: e.tensor_tensor(d_.t[:, :].rearrange("p (b h t) -> p b h t", b=16, h=8),
                                                  pden.t[:, :].rearrange("p (b h t) -> p b h t", b=16, h=8),
                                                  rawap(esinkP2.t, 0, [[8, 128], [0, 16], [1, 8], [0, 4]]), ALU.add), reads=[pden, esinkP2], writes=[d_])
            c.op("dve", lambda e: e.tensor_scalar(d_.t[:, :], d_.t[:, :], 2.0, None, ALU.mult), reads=[d_], writes=[d_])
            c.op("dve", lambda e: e.reciprocal(rd.t[:, :], d_.t[:, :]), reads=[d_], writes=[rd])
            c.op("dve", lambda e: e.tensor_tensor(o1.t[:, :], pvT.t[:, :], rd.t[:, :], ALU.mult), reads=[pvT, rd], writes=[o1])
            c.op("pool", lambda e: e.tensor_tensor(obT.t[:, 0:512].rearrange("p (h b t) -> p b h t", h=8, b=16),
                                                   o1.t[:, :].rearrange("p (b h t) -> p b h t", b=16, h=8),
                                                   zsT.rearrange("p (h b t) -> p b h t", h=8, b=16), ALU.mult), reads=[o1, kt], writes=[obT])

            yield

        oTb = banks[5]
        bstate["n"] = 3
        swa = swa_gen()
        for h in range(8):
            c.op("pe", lambda e, h=h: e.matmul(oTb.t[:, h * 64:(h + 1) * 64], v3v[:, 0, h * 128:(h + 1) * 128], aT.t[:, h, 0:64], start=(h == 0), stop=False, skip_group_check=True),
                 reads=[vv, aT], writes=[oTb], signal=False)
        for b_ in range(16):
            Sx = sx_list[b_ % 3]; dsx = sx_ds[b_ % 3]
            Sx3 = Sx.t[:, :].rearrange("p (h d) -> p h d", h=8)
            Sb = Sbf[b_ % 3]
            c.op("act", lambda e: e.activation(Sb.t[:, :, :], Sx3, AF.Copy), reads=[Sx], writes=[Sb])
            for h in range(8):
                last = (b_ == 15 and h == 7)
                c.op("pe", lambda e, h=h: e.matmul(oTb.t[:, h * 64 + b_ * 4:h * 64 + b_ * 4 + 4], Sb.t[:, h, :], qts[:, h * 64 + b_ * 4:h * 64 + b_ * 4 + 4],
                                                   start=False, stop=last, skip_group_check=True),
                     reads=[Sb, qt], writes=[oTb], signal=(h == 7))
            khm = oaT
            kho = (b_ % 2) * 1024
            c.op("act", lambda e: e.activation(khm.t[0:64, kho:kho + 1024], khtok, AF.Copy, scale=bmask.t[:, b_:b_ + 1]), reads=[qt, bmask], writes=[khm])
            for half in range(2):
                ps_ = P()
                for hh in range(4):
                    h = half * 4 + hh
                    c.op("pe", lambda e, h=h, hh=hh: e.matmul(ps_.t[:, hh * 128:(hh + 1) * 128], khm.t[0:64, kho + h * 128:kho + (h + 1) * 128], vs_[:, h * 128:(h + 1) * 128],
                                                             start=True, stop=True), reads=[khm, vv], writes=[ps_], signal=(hh == 3))
                P4b = bc(Pcs.rearrange("p (h b t) -> p h b t", h=8, b=16)[:, half * 4:half * 4 + 4, b_, 3], 128)
                Sh = Sx3[:, half * 4:half * 4 + 4, :]
                c.op("pool" if half == 0 else "dve", lambda e: e.tensor_tensor(Sh, Sh, P4b, ALU.mult), reads=[Sx] + S2, writes=[Sx])
                c.op("dve", lambda e: e.tensor_tensor(Sh, Sh, ps_.t[:, :].rearrange("p (h d) -> p h d", h=4), ALU.add), reads=[Sx, ps_], writes=[Sx])
            c.dma("act", ss_o[b_].rearrange("h p v -> p h v"), Sx3, dsx, reads=[Sx])
            if b_ + 2 < 16:
                issue_in(b_ + 2)
            for _ in range(3):
                next(swa, None)
        for _ in swa:
            pass
        bstate["n"] = 5
        sqT = PTt[0]
        c.op("act", lambda e: e.activation(sqT.t[:, :], oTb.t[:, :], AF.Square), reads=[oTb], writes=[sqT])
        pssq = P()
        c.op("pe", lambda e: e.matmul(pssq.t[:, :], ones128.t[:, :], sqT.t[:, :], start=True, stop=True), reads=[ones128, sqT], writes=[pssq])
        rs = TF(); rstd = TF(); o1 = TF()
        c.op("dve", lambda e: e.tensor_scalar(rs.t[:, :], pssq.t[:, :], 0.25 / 128, EPS, ALU.mult, ALU.add), reads=[pssq], writes=[rs])
        yN = TF(); tN = TF()
        c.op("dve", lambda e: e.tensor_scalar(yN.t[:, :].bitcast(I32), rs.t[:, :].bitcast(I32), -0.5, 1597463007.0, ALU.mult, ALU.add),
             reads=[rs], writes=[yN])
        for it in range(3):
            c.op("dve", lambda e: e.tensor_tensor(tN.t[:, :], yN.t[:, :], yN.t[:, :], ALU.mult), reads=[yN], writes=[tN])
            c.op("dve", lambda e: e.tensor_tensor(tN.t[:, :], tN.t[:, :], rs.t[:, :], ALU.mult), reads=[tN, rs], writes=[tN])
            c.op("dve", lambda e: e.tensor_scalar(tN.t[:, :], tN.t[:, :], -0.5, 1.5, ALU.mult, ALU.add), reads=[tN], writes=[tN])
            dstN = yN if it < 2 else rstd
            c.op("dve", lambda e: e.tensor_tensor(dstN.t[:, :], yN.t[:, :], tN.t[:, :], ALU.mult), reads=[yN, tN], writes=[dstN])
        c.op("dve", lambda e: e.tensor_tensor(o1.t[:, :], oTb.t[:, :], rstd.t[:, :], ALU.mult), reads=[oTb, rstd], writes=[o1])
        c.op("pool", lambda e: e.tensor_tensor(oaT.t[:, 0:512], o1.t[:, :], gT2, ALU.mult), reads=[o1] + S2, writes=[oaT])

        mark('s.p4')
        phase4(NS, 64, ysm, 0)
        mark('end')

    late_setup_a()
    cast_stage(0)
    for ti in range(NTL if STOP > 0 else 0):
        c.redirect = (ti == 0)
        prompt_tile(ti)
        c.redirect = False
    if STOP > 4 and NTL == NTILES:
        uT3e = v3(uT, 8)
        pv = P(); pk = P()
        kvout_t = F1K()
        kvout3 = kvout_t.t[:, 0:256].rearrange("p (a f) -> p a f", a=2)
        mm8(pv.t[:, 0:128], pv, lambda k: uT3e[:, k, 384:512], lambda k: wbv.t[:, k, :], [wbv, uT])
        for g in range(2):
            mm8(pk.t[:, g * 64:(g + 1) * 64], pk, lambda k: uT3e[:, k, 384:512], lambda k: wk2.t[:, k, g, 0, :], [wk2, uT])
        c.op("dve", lambda e: e.tensor_copy(kvout3[:, 1, :], pv.t[:, 0:128]), reads=[pv], writes=[kvout_t])
        c.op("act", lambda e: e.activation(kvout3[:, 0, :], pk.t[:, 0:128], AF.Copy), reads=[pk], writes=[kvout_t])
        d_kv = dyo[f1k.index(kvout_t)]
        c.dma("sp", kp_o[:, :], kvout3[:, 0, :], d_kv, reads=[kvout_t])
        c.dma("sp", vp_o[:, :], kvout3[:, 1, :], d_kv, reads=[kvout_t])
    c.dma("sp", sp_o[:, :, :].rearrange("h p v -> p h v"), S.t[:, :, :], c.dsem("spo"), reads=S2)
    sample_path()
    c.finish()
    return nc, c


def _rel_bucket(rel):
    n = np.maximum(rel, 0)
    nf = np.maximum(n, 1).astype(np.float32)
    large = 16 + (np.log(nf / np.float32(16)) / np.float32(math.log(128 / 16)) * np.float32(16)).astype(np.int32)
    large = np.minimum(large, 31)
    return np.where(n < 16, n, large)


def _consts():
    s = np.arange(128)[:, None]; t = np.arange(128)[None, :]
    mask64 = ((s // 64 == t // 64) & (s <= t)).astype(np.float32)
    scanm = np.zeros((128, 512), np.float32); scanm[:, ::64] = 1.0
    grev = np.zeros((32, 384), np.float32); valid = np.zeros((16, 384), np.float32)
    for i in range(384):
        rel = (383 - i) - 128
        if 0 <= rel < 128:
            grev[int(_rel_bucket(np.array(rel))), i] = 1.0
            valid[:, i] = 1.0
    p = np.arange(64)[:, None]
    bmask = (p // 4 == np.arange(16)[None, :]).astype(np.float32)
    s4 = np.arange(64)[:, None]; t4 = np.arange(64)[None, :]
    mask4 = ((s4 // 4 == t4 // 4) & (s4 <= t4)).astype(np.float32)
    return dict(k_ident=np.eye(128, dtype=np.float32), k_mask64=mask64, k_scanm=scanm, k_grev=grev, k_valid=valid,
                k_bmask=bmask, k_mask4=mask4)


_CACHE = {}


def kernel(x_prompt, x_sample, state_hgrn, cache_swa_k, cache_swa_v, p_prompt, p_sample,
           norm_pre, w_in, hgrn_lb, hgrn_norm, attn_sink, rel_bias, w_pa, w_pb, w_o,
           norm_post, w_ple, w_ple_gate):
    f = lambda a: np.ascontiguousarray(np.asarray(a), dtype=np.float32)
    x_prompt, x_sample, state_hgrn, cache_swa_k, cache_swa_v, p_prompt, p_sample = map(
        f, (x_prompt, x_sample, state_hgrn, cache_swa_k, cache_swa_v, p_prompt, p_sample))
    if "nc" not in _CACHE:
        _CACHE["nc"] = build_program()[0]
    nc = _CACHE["nc"]
    consts = _consts()
    shared = dict(w_in=f(w_in)[0], w_pa=f(w_pa)[0], w_pb=f(w_pb)[0], w_o=f(w_o)[0], w_pg=f(w_ple_gate)[0],
                  w_ple=f(w_ple)[0], norm_pre=f(norm_pre), hgrn_lb=f(hgrn_lb), hgrn_norm=f(hgrn_norm),
                  attn_sink=f(attn_sink), rel_bias=f(rel_bias), norm_post=f(norm_post), **consts)
    in_maps = []
    for i in range(8):
        b = slice(16 * i, 16 * i + 16)
        m = dict(shared)
        m.update(xp=x_prompt[i], pp=p_prompt[0, i], xsm=x_sample[b].reshape(64, 1024), psm=p_sample[0, b].reshape(64, 256),
                 st0=state_hgrn[0, b], ck=cache_swa_k[0, b].reshape(16, 128, 128), cv=cache_swa_v[0, b].reshape(16, 128, 128))
        in_maps.append(m)
    res = run_bass_kernel_spmd(nc, in_maps, core_ids=list(range(8)))
    R = res.results
    y_prompt = np.stack([R[i]["yp"] for i in range(8)])
    y_sample = np.concatenate([R[i]["ysm"].reshape(16, 4, 1024) for i in range(8)])
    s_prompt = np.stack([R[i]["sp_o"] for i in range(8)])[None]
    s_sample = np.concatenate([R[i]["ss_o"] for i in range(8)])[None]
    k_prompt = np.stack([R[i]["kp_o"].reshape(128, 2, 64) for i in range(8)])[None]
    v_prompt = np.stack([R[i]["vp_o"].reshape(128, 2, 64) for i in range(8)])[None]
    k_sample = np.concatenate([R[i]["ks_o"].reshape(16, 128, 2, 64) for i in range(8)])[None]
    v_sample = np.concatenate([R[i]["vs_o"].reshape(16, 128, 2, 64) for i in range(8)])[None]
    return tuple(np.ascontiguousarray(a, dtype=np.float32) for a in
                 (y_prompt, y_sample, s_prompt, s_sample, k_prompt, v_prompt, k_sample, v_sample))
```
